# Optimizing a Trainium2 kernel written in Bass

```python
import math
import jax, jax.numpy as jnp
from jax import lax
import numpy as np

D_MODEL = 4096
BATCH = 2
SEQ = 8192
DEPTH = 2

HEAD_DIM = 128
MIX_WIDTH = D_MODEL
N_HEADS_A = MIX_WIDTH // HEAD_DIM // 2
N_HEADS_B = MIX_WIDTH // HEAD_DIM - N_HEADS_A
DILATED_PATTERNS = ((128, 1), (512, 4), (2048, 16))
BLK = 128
N_BUCKETS = 32
MAX_DISTANCE = 2048
N_HEADS_C = MIX_WIDTH // HEAD_DIM
Q_LORA = 1024
KV_LORA = 512
QK_NOPE = 128
QK_ROPE = 64
V_DIM = 128
ROPE_THETA = 10000.0
EPS = 1e-6
NEG_INF = -1e30
N_EVEN = (DEPTH + 1) // 2
N_ODD = DEPTH // 2
IN_EVEN = 3 * N_HEADS_A * HEAD_DIM + 3 * N_HEADS_B * HEAD_DIM + MIX_WIDTH + N_HEADS_B
IN_ODD = Q_LORA + KV_LORA + QK_ROPE + MIX_WIDTH

kernel_name = 'hybrid_dilated_fox_mla_adaln'


def rmsnorm(x, g):
    xf = x.astype(jnp.float32)
    y = xf * lax.rsqrt(jnp.mean(xf * xf, axis=-1, keepdims=True) + EPS)
    return (y * g.astype(jnp.float32)).astype(x.dtype)


def t5_bucket(n):
    max_exact = N_BUCKETS // 2
    nf = jnp.maximum(n, 1).astype(jnp.float32)
    large = max_exact + (jnp.log(nf / max_exact) / math.log(MAX_DISTANCE / max_exact)
                         * (N_BUCKETS - max_exact)).astype(jnp.int32)
    large = jnp.minimum(large, N_BUCKETS - 1)
    return jnp.where(n < max_exact, n, large)


def rope(x, positions):
    half = x.shape[-1] // 2
    inv_freq = ROPE_THETA ** (-jnp.arange(half, dtype=jnp.float32) / half)
    ang = positions.astype(jnp.float32)[..., None] * inv_freq
    cos = jnp.cos(ang)[:, :, None, :]
    sin = jnp.sin(ang)[:, :, None, :]
    x1 = x[..., :half].astype(jnp.float32)
    x2 = x[..., half:].astype(jnp.float32)
    return jnp.concatenate([x1 * cos - x2 * sin, x1 * sin + x2 * cos], axis=-1).astype(x.dtype)


def dilated_window_attention(q, k, v, rel_bias, window, dilation):
    B, S, H, Dh = q.shape
    steps = window // dilation
    span = dilation * BLK
    s_pad = -(-S // span) * span
    L = s_pad // dilation
    nb = L // BLK

    def by_residue(t):
        t = jnp.pad(t, ((0, 0), (0, s_pad - S), (0, 0), (0, 0)))
        t = t.reshape(B, L, dilation, H, Dh).transpose(0, 3, 2, 1, 4)
        return t.reshape(B, H, dilation, nb, BLK, Dh)

    def with_prev(t):
        prev = jnp.pad(t, ((0, 0), (0, 0), (0, 0), (1, 0), (0, 0), (0, 0)))[:, :, :, :-1]
        return jnp.concatenate([prev, t], axis=4)

    qr = by_residue(q)
    kb = with_prev(by_residue(k))
    vb = with_prev(by_residue(v))
    s = jnp.einsum('bhrnqd,bhrnkd->bhrnqk', qr, kb).astype(jnp.float32) * (HEAD_DIM ** -0.5)
    i = jnp.arange(BLK)[:, None]
    j = jnp.arange(2 * BLK)[None, :]
    step_dist = i + BLK - j
    band = (step_dist >= 0) & (step_dist <= steps)
    valid = band[None] & ((jnp.arange(nb)[:, None, None] > 0) | (j[None] >= BLK))
    bucket = t5_bucket(jnp.maximum(step_dist, 0) * dilation)
    bias = jnp.moveaxis(rel_bias[bucket].astype(jnp.float32), -1, 0)
    s = s + bias[None, :, None, None]
    s = jnp.where(valid[None, None, None], s, NEG_INF)
    m = jnp.max(s, axis=-1, keepdims=True)
    p = jnp.exp(s - m)
    denom = jnp.sum(p, axis=-1, keepdims=True)
    o = jnp.einsum('bhrnqk,bhrnkd->bhrnqd', (p / denom).astype(v.dtype), vb)
    lse = m[..., 0] + jnp.log(denom[..., 0])
    o = o.reshape(B, H, dilation, L, Dh).transpose(0, 3, 2, 1, 4).reshape(B, s_pad, H, Dh)[:, :S]
    lse = lse.reshape(B, H, dilation, L).transpose(0, 3, 2, 1).reshape(B, s_pad, H)[:, :S]
    return o, lse


def causal_block_attention(q, k, v, scale, log_forget_cum=None):
    B, S, H, Dk = q.shape
    nb = S // BLK
    q_blocks = q.reshape(B, nb, BLK, H, Dk).swapaxes(0, 1)
    k_pos = jnp.arange(S)
    if log_forget_cum is not None:
        F = log_forget_cum.transpose(0, 2, 1)
        F_blocks = F.reshape(B, H, nb, BLK).transpose(2, 0, 1, 3)
        xs = (jnp.arange(nb), q_blocks, F_blocks)
    else:
        xs = (jnp.arange(nb), q_blocks)

    def block(xs_n):
        n, q_blk = xs_n[0], xs_n[1]
        s = jnp.einsum('bqhd,bkhd->bhqk', q_blk, k).astype(jnp.float32) * scale
        if log_forget_cum is not None:
            s = s + (xs_n[2][..., :, None] - F[:, :, None, :])
        q_pos = n * BLK + jnp.arange(BLK)
        s = jnp.where(k_pos[None, :] <= q_pos[:, None], s, NEG_INF)
        p = jax.nn.softmax(s, axis=-1).astype(v.dtype)
        return jnp.einsum('bhqk,bkhd->bqhd', p, v)

    out = lax.map(block, xs)
    return out.swapaxes(0, 1).reshape(B, S, H, v.shape[-1])


def even_mixer(h, rel_bias, w_in, b_f, w_out):
    B, S, _ = h.shape
    wa = N_HEADS_A * HEAD_DIM
    wb = N_HEADS_B * HEAD_DIM
    proj = h @ w_in
    qa, ka, va, qb, kb, vb, z, f_logit = jnp.split(
        proj, list(np.cumsum([wa, wa, wa, wb, wb, wb, MIX_WIDTH])), axis=-1)
    ha = lambda t: t.reshape(B, S, N_HEADS_A, HEAD_DIM)
    hb = lambda t: t.reshape(B, S, N_HEADS_B, HEAD_DIM)
    outs, lses = [], []
    for window, dilation in DILATED_PATTERNS:
        o, l = dilated_window_attention(ha(qa), ha(ka), ha(va), rel_bias, window, dilation)
        outs.append(o)
        lses.append(l)
    mix_w = jax.nn.softmax(jnp.stack(lses, axis=0), axis=0)
    o_a = jnp.einsum('pbsh,pbshd->bshd', mix_w.astype(h.dtype), jnp.stack(outs, axis=0))
    log_f = jax.nn.log_sigmoid(f_logit.astype(jnp.float32) + b_f.astype(jnp.float32))
    F = jnp.cumsum(log_f, axis=1)
    o_b = causal_block_attention(hb(qb), hb(kb), hb(vb), HEAD_DIM ** -0.5, F)
    o = jnp.concatenate([o_a.reshape(B, S, wa), o_b.reshape(B, S, wb)], axis=-1)
    return (o * jax.nn.silu(z)) @ w_out


def odd_mixer(h, positions, w_in, g_q, g_kv, w_qb, w_kvb, w_out):
    B, S, _ = h.shape
    proj = h @ w_in
    cq, ckv, k_r, z = jnp.split(proj, list(np.cumsum([Q_LORA, KV_LORA, QK_ROPE])), axis=-1)
    q = (rmsnorm(cq, g_q) @ w_qb).reshape(B, S, N_HEADS_C, QK_NOPE + QK_ROPE)
    kv = (rmsnorm(ckv, g_kv) @ w_kvb).reshape(B, S, N_HEADS_C, QK_NOPE + V_DIM)
    q_nope, q_rope = q[..., :QK_NOPE], rope(q[..., QK_NOPE:], positions)
    k_nope, v = kv[..., :QK_NOPE], kv[..., QK_NOPE:]
    k_rope = rope(k_r[:, :, None, :], positions)
    q = jnp.concatenate([q_nope, q_rope], axis=-1)
    k = jnp.concatenate([k_nope, jnp.broadcast_to(k_rope, (B, S, N_HEADS_C, QK_ROPE))], axis=-1)
    o = causal_block_attention(q, k, v, (QK_NOPE + QK_ROPE) ** -0.5)
    return (o.reshape(B, S, N_HEADS_C * V_DIM) * jax.nn.silu(z)) @ w_out


def setup_inputs(seed: int = 0) -> dict:
    key = jax.random.key(seed)
    ks = jax.random.split(key, 18)
    nrm = lambda k, shape, s: jax.random.normal(k, shape, jnp.float32) * s
    x = nrm(ks[0], (BATCH, SEQ, D_MODEL), 1.0)
    c = nrm(ks[1], (BATCH, D_MODEL), 1.0)
    positions = (jax.random.randint(ks[2], (BATCH, 1), 0, 4096, dtype=jnp.int32)
                 + jnp.arange(SEQ, dtype=jnp.int32)[None, :])
    g_norm = 1.0 + nrm(ks[3], (DEPTH, D_MODEL), 0.02)
    w_ada = nrm(ks[4], (DEPTH, D_MODEL, 3 * D_MODEL), 0.1 * D_MODEL ** -0.5)
    b_ada = nrm(ks[5], (DEPTH, 3 * D_MODEL), 0.02) + jnp.concatenate(
        [jnp.zeros((2 * D_MODEL,), jnp.float32), jnp.ones((D_MODEL,), jnp.float32)])[None]
    rel_bias = nrm(ks[6], (N_BUCKETS, N_HEADS_A), 0.5)
    w_in_even = nrm(ks[7], (N_EVEN, D_MODEL, IN_EVEN), D_MODEL ** -0.5)
    b_forget = jnp.linspace(1.0, 5.0, N_HEADS_B, dtype=jnp.float32)[None] + nrm(ks[8], (N_EVEN, N_HEADS_B), 0.1)
    w_out_even = nrm(ks[9], (N_EVEN, MIX_WIDTH, D_MODEL), MIX_WIDTH ** -0.5)
    w_in_odd = nrm(ks[10], (N_ODD, D_MODEL, IN_ODD), D_MODEL ** -0.5)
    g_q_lora = 1.0 + nrm(ks[11], (N_ODD, Q_LORA), 0.02)
    g_kv_lora = 1.0 + nrm(ks[12], (N_ODD, KV_LORA), 0.02)
    w_q_b = nrm(ks[13], (N_ODD, Q_LORA, N_HEADS_C * (QK_NOPE + QK_ROPE)), Q_LORA ** -0.5)
    w_kv_b = nrm(ks[14], (N_ODD, KV_LORA, N_HEADS_C * (QK_NOPE + V_DIM)), KV_LORA ** -0.5)
    w_out_odd = nrm(ks[15], (N_ODD, MIX_WIDTH, D_MODEL), MIX_WIDTH ** -0.5)
    g_final = 1.0 + nrm(ks[16], (D_MODEL,), 0.02)
    return {'x': x, 'c': c, 'positions': positions, 'g_norm': g_norm, 'w_ada': w_ada,
            'b_ada': b_ada, 'rel_bias': rel_bias, 'w_in_even': w_in_even, 'b_forget': b_forget,
            'w_out_even': w_out_even, 'w_in_odd': w_in_odd, 'g_q_lora': g_q_lora,
            'g_kv_lora': g_kv_lora, 'w_q_b': w_q_b, 'w_kv_b': w_kv_b, 'w_out_odd': w_out_odd,
            'g_final': g_final}


def reference(x, c, positions, g_norm, w_ada, b_ada, rel_bias, w_in_even, b_forget, w_out_even,
              w_in_odd, g_q_lora, g_kv_lora, w_q_b, w_kv_b, w_out_odd, g_final):
    for layer in range(DEPTH):
        mod = jax.nn.silu(c) @ w_ada[layer] + b_ada[layer]
        shift, scale, gate = jnp.split(mod, 3, axis=-1)
        h = rmsnorm(x, g_norm[layer]) * (1.0 + scale[:, None, :]) + shift[:, None, :]
        i = layer // 2
        if layer % 2 == 0:
            y = even_mixer(h, rel_bias, w_in_even[i], b_forget[i], w_out_even[i])
        else:
            y = odd_mixer(h, positions, w_in_odd[i], g_q_lora[i], g_kv_lora[i],
                          w_q_b[i], w_kv_b[i], w_out_odd[i])
        x = x + gate[:, None, :] * y
    return rmsnorm(x, g_final)
```

```python
import contextlib
import math
import numpy as np
import ml_dtypes
import concourse.bass as bass
import concourse.mybir as mybir
from concourse.bass_utils import run_bass_kernel_spmd

F32 = mybir.dt.float32
BF16 = mybir.dt.bfloat16
I32 = mybir.dt.int32
AF = mybir.ActivationFunctionType
ALU = mybir.AluOpType
NPBF = ml_dtypes.bfloat16

D = 4096
S = 8192
NB = 2
KC = D // 128
T = 2048
NCORES = 8
EPS = 1e-6
NEG = -30000.0
DEBUG = {}


class KB:
    def __init__(self, nc):
        self.nc = nc
        self.es = contextlib.ExitStack()
        self.root_es = self.es
        self.eng = {"pe": nc.tensor, "act": nc.scalar, "dve": nc.vector, "pool": nc.gpsimd, "sp": nc.sync}
        self.cur = {}
        self.cnt = {}
        self.waited = {}
        self.nsem = 0
        self.bank_free = [None] * 8
        for e in self.eng:
            self._fresh(e)

    def _fresh(self, e):
        self.nsem += 1
        self.cur[e] = self.root_es.enter_context(self.nc.semaphore(f"t_{e}_{self.nsem}"))
        self.cnt[e] = 0

    def sem(self, name):
        self.nsem += 1
        return self.root_es.enter_context(self.nc.semaphore(f"{name}_{self.nsem}"))

    def sbuf(self, name, shape, dt):
        return self.es.enter_context(self.nc.sbuf_tensor(name, list(shape), dt))

    def psum(self, name, shape, dt=F32):
        return self.es.enter_context(self.nc.psum_tensor(name, list(shape), dt))

    def sig(self, e, instr):
        if self.cnt[e] >= 30000:
            self._fresh(e)
        self.cnt[e] += 1
        instr.then_inc(self.cur[e], 1)
        self.last = getattr(self, "last", {})
        self.last[e] = (self.cur[e], self.cnt[e], e)
        return self.last[e]

    def barrier(self, *extra):
        tks = [t for t in getattr(self, "last", {}).values()] + list(extra)
        for e in self.eng:
            self.wait(e, *tks)

    def wait(self, e, *tickets):
        for t in tickets:
            if t is None:
                continue
            sem, v = t[0], t[1]
            key = (e, id(sem))
            if self.waited.get(key, 0) >= v:
                continue
            self.waited[key] = v
            self.eng[e].wait_ge(sem, v)

    def close(self):
        self.root_es.close()


class DmaBuf:
    def __init__(self, kb, name):
        self.kb = kb
        self.sem = kb.sem("d_" + name)
        self.count = 0

    def dma(self, e, out, in_):
        self.kb.eng[e].dma_start(out=out, in_=in_).then_inc(self.sem, 16)
        self.count += 16

    def ticket(self):
        return (self.sem, self.count, "dma")


def _launch(nc, in_maps):
    if DEBUG.get("trace"):
        res = run_bass_kernel_spmd(nc, in_maps, core_ids=list(range(NCORES)), trace=True)
        DEBUG.setdefault("times", []).append(res.exec_time_ns)
        print("LAUNCH exec_time_ns", res.exec_time_ns, flush=True)
        return res
    return run_bass_kernel_spmd(nc, in_maps, core_ids=list(range(NCORES)))


def _bf16_round(a):
    return a.astype(NPBF)


def build_mod():
    nc = bass.Bass("TRN2", target_bir_lowering=False)
    NCOL = 3072
    cT = nc.dram_tensor("cT", [128, KC * NB], F32, kind="ExternalInput").ap()
    wada = nc.dram_tensor("wada", [6, 128, KC * 512], F32, kind="ExternalInput").ap()
    bada = nc.dram_tensor("bada", [NB, NCOL], F32, kind="ExternalInput").ap()
    out = nc.dram_tensor("modp", [NB, NCOL], F32, kind="ExternalOutput").ap()
    kb = KB(nc)
    c_in = kb.sbuf("c_in", [128, KC * NB], F32)
    sc = kb.sbuf("sc", [128, KC * NB], F32)
    bsb = kb.sbuf("bsb", [NB, NCOL], F32)
    osb = kb.sbuf("osb", [NB, NCOL], F32)
    wb = [kb.sbuf(f"wb{i}", [128, KC * 512], F32) for i in range(2)]
    ps = [kb.psum(f"ps{i}", [128, 512]) for i in range(2)]
    ld0 = DmaBuf(kb, "c")
    ld0.dma("sp", c_in[:], cT)
    ld0.dma("sp", bsb[:], bada)
    wl = [DmaBuf(kb, f"w{i}") for i in range(2)]
    kb.wait("act", ld0.ticket())
    t_sc = kb.sig("act", nc.scalar.activation(out=sc[:], in_=c_in[:], func=AF.Silu))
    pe_done = [None, None]
    ev_done = [None, None]
    for j in range(6):
        b = j % 2
        kb.wait("sp", pe_done[b])
        wl[b].dma("sp", wb[b][:], wada[j])
        if j == 0:
            kb.wait("pe", t_sc)
        kb.wait("pe", wl[b].ticket(), ev_done[b])
        for kc in range(KC):
            mm = nc.tensor.matmul(ps[b][0:NB, :], lhsT=sc[:, kc * NB:(kc + 1) * NB],
                                  rhs=wb[b][:, kc * 512:(kc + 1) * 512], start=(kc == 0), stop=(kc == KC - 1))
        pe_done[b] = kb.sig("pe", mm)
        kb.wait("dve", pe_done[b], ld0.ticket())
        ev_done[b] = kb.sig("dve", nc.vector.tensor_tensor(out=osb[:, j * 512:(j + 1) * 512], in0=ps[b][0:NB, :],
                                                            in1=bsb[:, j * 512:(j + 1) * 512], op=ALU.add))
    st = DmaBuf(kb, "st")
    kb.wait("sp", ev_done[0], ev_done[1])
    st.dma("sp", out, osb[:])
    kb.wait("sp", st.ticket())
    kb.close()
    return nc


def run_mod(c, w_ada, b_ada):
    cT = np.ascontiguousarray(c.reshape(NB, KC, 128).transpose(2, 1, 0).reshape(128, KC * NB))
    in_maps = []
    for core in range(NCORES):
        l, q = core // 4, core % 4
        w = w_ada[l][:, q * 3072:(q + 1) * 3072]
        w = w.reshape(KC, 128, 6, 512).transpose(2, 1, 0, 3).reshape(6, 128, KC * 512)
        b = np.broadcast_to(b_ada[l][None, q * 3072:(q + 1) * 3072], (NB, 3072))
        in_maps.append({"cT": cT, "wada": np.ascontiguousarray(w), "bada": np.ascontiguousarray(b)})
    nc = build_mod()
    res = _launch(nc, in_maps)
    mod = np.zeros((2, NB, 3 * D), np.float32)
    for core in range(NCORES):
        l, q = core // 4, core % 4
        mod[l, :, q * 3072:(q + 1) * 3072] = res.results[core]["modp"]
    return mod


def emit_norm(kb, xT_dram, hT, ones_bf, A, B, pbank, tag, sq_scale=1.0 / D):
    nc = kb.nc
    nkc = hT.shape[1]
    TN = hT.shape[2]
    TT = TN // 512
    xv = xT_dram.rearrange("(k p) t -> k p t", p=128)
    xb = [kb.sbuf(f"{tag}_x{i}", [128, 512], F32) for i in range(3)]
    xl = [DmaBuf(kb, f"{tag}_x{i}") for i in range(3)]
    x_free = [None] * 3
    sq = [kb.sbuf(f"{tag}_sq{i}", [128, 512], BF16) for i in range(2)]
    sq_free = [None] * 2
    tmp = [kb.sbuf(f"{tag}_tmp{i}", [128, 512], F32) for i in range(2)]
    tmp_free = [None] * 2
    rstd = kb.sbuf(f"{tag}_rstd", [128, TN], F32)
    sd = kb.sbuf(f"{tag}_sd", [128, 512], F32)
    n = 0
    out_t = []
    for tt in range(TT):
        tsl = slice(tt * 512, (tt + 1) * 512)
        for kc in range(nkc):
            i = n % 3
            n += 1
            kb.wait("sp", x_free[i])
            xl[i].dma("sp", xb[i][:], xv[kc, :, tsl])
            j = kc % 2
            kb.wait("act", xl[i].ticket(), sq_free[j])
            t_sq = kb.sig("act", nc.scalar.activation(out=sq[j][:], in_=xb[i][:], func=AF.Square))
            x_free[i] = t_sq
            kb.wait("pe", t_sq)
            if kc == 0 and tt > 0:
                kb.wait("pe", t_sd)
            mm = nc.tensor.matmul(pbank[:, :], lhsT=ones_bf[:, :], rhs=sq[j][:], start=(kc == 0), stop=(kc == nkc - 1))
            sq_free[j] = kb.sig("pe", mm)
        t_ss = sq_free[(nkc - 1) % 2]
        kb.wait("act", t_ss)
        t_sd = kb.sig("act", nc.scalar.activation(out=sd[:], in_=pbank[:, :], func=AF.Sqrt, scale=sq_scale, bias=EPS))
        kb.wait("dve", t_sd)
        t_r = kb.sig("dve", nc.vector.reciprocal(out=rstd[:, tsl], in_=sd[:]))
        for kc in range(nkc):
            i = n % 3
            n += 1
            kb.wait("sp", x_free[i])
            xl[i].dma("sp", xb[i][:], xv[kc, :, tsl])
            j = kc % 2
            kb.wait("dve", xl[i].ticket(), tmp_free[j], t_r)
            t_m = kb.sig("dve", nc.vector.scalar_tensor_tensor(out=tmp[j][:], in0=xb[i][:], scalar=A[:, kc:kc + 1],
                                                                in1=rstd[:, tsl], op0=ALU.mult, op1=ALU.mult))
            x_free[i] = t_m
            kb.wait("act", t_m)
            t_h = kb.sig("act", nc.scalar.activation(out=hT[:, kc, tsl], in_=tmp[j][:], func=AF.Identity,
                                                     bias=B[:, kc:kc + 1]))
            tmp_free[j] = t_h
        out_t.append(t_h)
    kb.norm_bank_ticket = t_sd
    return out_t


def emit_linear_fm(kb, w_dram, nchunks, hT, nkc, banks, evac, tag, h_ready=None, mrows=128, wbufs=None, bank_ids=None):
    nc = kb.nc
    TT = hT.shape[2] // 512
    if wbufs is None:
        wbufs = [kb.sbuf(f"{tag}_w{i}", [128, nkc * mrows], BF16) for i in range(2)]
    wl = [DmaBuf(kb, f"{tag}_w{i}") for i in range(2)]
    w_free = kb.__dict__.setdefault("wfree", {}).setdefault(id(wbufs[0]), [None, None])
    bank_free = kb.bank_free
    if bank_ids is None:
        bank_ids = list(range(8))

    def load(c):
        b = c % 2
        kb.wait("pool", w_free[b])
        wl[b].dma("pool", wbufs[b][:, 0:nkc * mrows], w_dram[c])

    load(0)
    for c in range(nchunks):
        b = c % 2
        if c + 1 < nchunks:
            load(c + 1)
        kb.wait("pe", wl[b].ticket())
        for tt in range(TT):
            bk = bank_ids[(c * TT + tt) % len(bank_ids)]
            kb.wait("pe", bank_free[bk])
            if h_ready is not None:
                kb.wait("pe", h_ready[tt])
            for kc in range(nkc):
                mm = nc.tensor.matmul(banks[bk][0:mrows, :], lhsT=wbufs[b][:, kc * mrows:(kc + 1) * mrows],
                                      rhs=hT[:, kc, tt * 512:(tt + 1) * 512], start=(kc == 0), stop=(kc == nkc - 1))
            t_mm = kb.sig("pe", mm)
            bank_free[bk] = evac(c, tt, banks[bk], t_mm)
        w_free[b] = t_mm
    kb.bank_free_last_mm = t_mm
    kb.w_free_tickets = list(w_free)
    return bank_free


class OutStage:
    def __init__(self, kb, name, shape, dt, n=2):
        self.kb = kb
        self.bufs = [kb.sbuf(f"{name}{i}", shape, dt) for i in range(n)]
        self.dmas = [DmaBuf(kb, f"{name}{i}") for i in range(n)]
        self.n = n
        self.k = -1

    def next(self):
        self.k += 1
        i = self.k % self.n
        return self.bufs[i], self.dmas[i].ticket()

    def store(self, dst, src, *tickets):
        i = self.k % self.n
        self.kb.wait("sp", *tickets)
        self.dmas[i].dma("sp", dst, src)

    def all_tickets(self):
        return [d.ticket() for d in self.dmas]


def emit_linear_tm(kb, w_dram, ntiles, hT, nkc, NW, banks, evac, tag, wbufs, bank_ids=None):
    nc = kb.nc
    NT = hT.shape[2] // 128
    wl = [DmaBuf(kb, f"{tag}_w{i}") for i in range(2)]
    w_free = kb.__dict__.setdefault("wfree", {}).setdefault(id(wbufs[0]), [None, None])
    bank_free = kb.bank_free
    if bank_ids is None:
        bank_ids = list(range(8))
    n = 0

    def load(j):
        b = j % 2
        kb.wait("pool", w_free[b])
        wl[b].dma("pool", wbufs[b][:, 0:nkc * NW], w_dram[j])

    load(0)
    for j in range(ntiles):
        b = j % 2
        if j + 1 < ntiles:
            load(j + 1)
        kb.wait("pe", wl[b].ticket())
        for t in range(NT):
            bk = bank_ids[n % len(bank_ids)]
            n += 1
            kb.wait("pe", bank_free[bk])
            for kc in range(nkc):
                mm = nc.tensor.matmul(banks[bk][:, 0:NW], lhsT=hT[:, kc, t * 128:(t + 1) * 128],
                                      rhs=wbufs[b][:, kc * NW:(kc + 1) * NW], start=(kc == 0), stop=(kc == nkc - 1))
            t_mm = kb.sig("pe", mm)
            bank_free[bk] = evac(j, t, banks[bk], t_mm)
        w_free[b] = t_mm
    kb.bank_free_last_mm = t_mm
    kb.w_free_tickets = list(w_free)
    return bank_free


def load_consts(kb, aps, eng="sp"):
    d = DmaBuf(kb, "consts")
    for o, i in aps:
        d.dma(eng, o, i)
    return d.ticket()


NQK0 = 64
NZ = 32


def build_pre0():
    nc = bass.Bass("TRN2", target_bir_lowering=False)
    xT = nc.dram_tensor("xT", [D, T], F32, kind="ExternalInput").ap()
    modT = nc.dram_tensor("modT", [128, 96], F32, kind="ExternalInput").ap()
    gT = nc.dram_tensor("gT", [128, KC], F32, kind="ExternalInput").ap()
    ones_in = nc.dram_tensor("ones", [128, 128], F32, kind="ExternalInput").ap()
    w_fm = nc.dram_tensor("w_fm", [DEBUG.get("pre0_nfm", NQK0 + NZ), 128, KC * 128], F32, kind="ExternalInput").ap()
    w_f = nc.dram_tensor("w_f", [1, 128, KC * 16], F32, kind="ExternalInput").ap()
    w_tm = nc.dram_tensor("w_tm", [16, 128, KC * 256], F32, kind="ExternalInput").ap()
    qkT = nc.dram_tensor("qkT", [NQK0 * 128, T], BF16, kind="ExternalOutput").ap()
    zT = nc.dram_tensor("zT", [D, T], BF16, kind="ExternalOutput").ap()
    v = nc.dram_tensor("v", [T, D], BF16, kind="ExternalOutput").ap()
    fT = nc.dram_tensor("fT", [16, T], F32, kind="ExternalOutput").ap()
    kb = KB(nc)
    hT = kb.sbuf("hT", [128, KC, T], BF16)
    ones_bf = kb.sbuf("ones_bf", [128, 128], BF16)
    mod_sb = kb.sbuf("mod_sb", [128, 96], F32)
    g_sb = kb.sbuf("g_sb", [128, KC], F32)
    A = kb.sbuf("A", [128, KC], F32)
    banks = [kb.psum(f"bank{i}", [128, 512]) for i in range(8)]
    t_c = load_consts(kb, [(mod_sb[:], modT), (g_sb[:], gT)])
    dc = DmaBuf(kb, "ones")
    dc.dma("pool", ones_bf[:], ones_in)
    kb.wait("dve", t_c)
    t_A = kb.sig("dve", nc.vector.scalar_tensor_tensor(out=A[:], in0=mod_sb[:, 32:64], scalar=1.0, in1=g_sb[:],
                                                        op0=ALU.add, op1=ALU.mult))
    kb.wait("act", t_c)
    kb.wait("pe", dc.ticket())
    wbufs = [kb.sbuf(f"fm_w{i}", [128, KC * 128], BF16) for i in range(2)]
    stage = OutStage(kb, "fmst", [128, T], BF16)
    fstage = kb.sbuf("fstage", [16, T], F32)
    with contextlib.ExitStack() as sub:
        kb_es, kb.es = kb.es, sub
        h_ready = emit_norm(kb, xT, hT, ones_bf, A, mod_sb[:, 0:32], banks[0], "n0")
        kb.es = kb_es
    kb.bank_free[0] = kb.norm_bank_ticket
    tmw = [kb.sbuf(f"tm_w{i}", [128, KC * 256], BF16) for i in range(2)]
    vstage = OutStage(kb, "vst", [128, 16, 256], BF16, n=1)

    state = {}
    qscale = 1.0 / math.sqrt(128.0)

    def evac_fm(c, tt, bank, t_mm):
        if tt == 0:
            buf, t_prev = stage.next()
            state["buf"] = buf
            state["prev"] = t_prev
        buf = state["buf"]
        sl = slice(tt * 512, (tt + 1) * 512)
        if c >= NQK0:
            e = "act"
            kb.wait(e, t_mm, state["prev"])
            tk = kb.sig(e, nc.scalar.activation(out=buf[:, sl], in_=bank[:, :], func=AF.Silu))
        else:
            sc = qscale if (c < 16 or 32 <= c < 48) else 1.0
            e = "dve" if (tt % 2 == 0) else "act"
            kb.wait(e, t_mm, state["prev"])
            if e == "dve":
                tk = kb.sig(e, nc.vector.tensor_scalar(out=buf[:, sl], in0=bank[:, :], scalar1=sc, scalar2=None,
                                                       op0=ALU.mult))
            else:
                tk = kb.sig(e, nc.scalar.activation(out=buf[:, sl], in_=bank[:, :], func=AF.Copy, scale=sc))
        state.setdefault("tks", []).append(tk)
        if tt == 3:
            dst = qkT[c * 128:(c + 1) * 128, :] if c < NQK0 else zT[(c - NQK0) * 128:(c - NQK0 + 1) * 128, :]
            stage.store(dst, buf[:, :], *state["tks"])
            state["tks"] = []
        return tk

    if DEBUG.get("pre0", 9) >= 2:
        emit_linear_fm(kb, w_fm, DEBUG.get("pre0_nfm", NQK0 + NZ), hT, KC, banks, evac_fm, "fm", h_ready=h_ready, wbufs=wbufs)

    fw = [kb.sbuf(f"f_w{i}", [128, KC * 16], BF16) for i in range(2)]
    ftk = []

    def evac_f(c, tt, bank, t_mm):
        kb.wait("dve", t_mm)
        tk = kb.sig("dve", nc.vector.tensor_copy(out=fstage[:, tt * 512:(tt + 1) * 512], in_=bank[0:16, :]))
        ftk.append(tk)
        return tk

    fst = DmaBuf(kb, "fst")
    if DEBUG.get("pre0", 9) >= 3:
        emit_linear_fm(kb, w_f, 1, hT, KC, banks, evac_f, "ff", mrows=16, wbufs=fw)
        kb.wait("sp", *ftk)
        fst.dma("sp", fT, fstage[:, :])

    vs = {}

    def evac_tm(j, t, bank, t_mm):
        if t == 0:
            buf, t_prev = vstage.next()
            vs["buf"], vs["prev"], vs["tks"] = buf, t_prev, []
        e = "dve" if (t % 2 == 0) else "act"
        kb.wait(e, t_mm, vs["prev"])
        if e == "dve":
            tk = kb.sig(e, nc.vector.tensor_copy(out=vs["buf"][:, t, :], in_=bank[:, 0:256]))
        else:
            tk = kb.sig(e, nc.scalar.copy(out=vs["buf"][:, t, :], in_=bank[:, 0:256]))
        vs["tks"].append(tk)
        if t == 15:
            dst = v[:, j * 256:(j + 1) * 256].rearrange("(t p) c -> p t c", p=128)
            vstage.store(dst, vs["buf"][:, :, :], *vs["tks"])
        return tk

    if DEBUG.get("pre0", 9) >= 4:
        emit_linear_tm(kb, w_tm, 16, hT, KC, 256, banks, evac_tm, "tm", tmw)
    kb.wait("sp", fst.ticket(), *stage.all_tickets(), *vstage.all_tickets())
    kb.close()
    return nc


def lay_fm(w, ncols_chunk=128):
    K_, N_ = w.shape
    kc = K_ // 128
    c = N_ // ncols_chunk
    return np.ascontiguousarray(w.reshape(kc, 128, c, ncols_chunk).transpose(2, 1, 0, 3).reshape(c, 128, kc * ncols_chunk))


def tok_shard_T(x):
    out = []
    for core in range(NCORES):
        b, q = core // 4, core % 4
        out.append(np.ascontiguousarray(x[b, q * T:(q + 1) * T, :].T))
    return out


def colT(vec):
    return np.ascontiguousarray(vec.reshape(-1, 128).T)


def run_pre0(x, mod0, g0, w_in):
    w_qk = w_in[:, 0:12288].reshape(D, 6, 2048)
    w_fm = np.concatenate([w_qk[:, 0], w_qk[:, 1], w_qk[:, 3], w_qk[:, 4], w_in[:, 12288:16384]], axis=1)
    w_tm = np.concatenate([w_qk[:, 2], w_qk[:, 5]], axis=1)
    w_fm_l = lay_fm(w_fm, 128)
    w_tm_l = lay_fm(w_tm, 256)
    w_f_l = lay_fm(w_in[:, 16384:16400], 16)
    ones = np.ones((128, 128), np.float32)
    gT = colT(g0)
    xs = tok_shard_T(x)
    in_maps = []
    for core in range(NCORES):
        b = core // 4
        in_maps.append({"xT": xs[core], "modT": colT(mod0[b]), "gT": gT, "ones": ones,
                        "w_fm": w_fm_l[:DEBUG.get("pre0_nfm", NQK0 + NZ)], "w_f": w_f_l, "w_tm": w_tm_l})
    nc = build_pre0()
    res = _launch(nc, in_maps)
    return res.results


class Stream:
    def __init__(self, kb, sbanks, pbufs, LA=2):
        self.kb, self.sbanks, self.pbufs, self.LA = kb, sbanks, pbufs, LA
        self.s_free = [None] * len(sbanks)
        self.p_free = [None] * len(pbufs)
        self.n = 0
        self.pending = []
        self.deferred = []

    def _run_deferred(self, force=False):
        keep = []
        for due, fn in self.deferred:
            if force or due <= self.n:
                fn()
            else:
                keep.append((due, fn))
        self.deferred = keep

    def defer(self, delay, fn):
        self.deferred.append((self.n + delay, fn))

    def _pv(self):
        kb = self.kb
        i, tile, t_act = self.pending.pop(0)
        pb = i % len(self.pbufs)
        kb.wait("pe", t_act, *tile.get("pv_waits", lambda: [])())
        mm = tile["pv"](self.pbufs[pb])
        t_pv = kb.sig("pe", mm)
        self.p_free[pb] = t_pv
        self.last_pv = t_pv
        if "post" in tile:
            tile["post"](t_pv)

    def push(self, tile):
        kb = self.kb
        i = self.n
        b = i % len(self.sbanks)
        pb = i % len(self.pbufs)
        self._run_deferred()
        kb.wait("pe", self.s_free[b], *tile.get("qk_waits", []))
        mm = tile["qk"](self.sbanks[b])
        t_qk = kb.sig("pe", mm)
        kb.wait("act", t_qk, self.p_free[pb], *tile.get("act_waits", []))
        a = tile["act"](self.sbanks[b], self.pbufs[pb])
        t_act = kb.sig("act", a)
        self.s_free[b] = t_act
        self.pending.append((i, tile, t_act))
        self.n += 1
        if len(self.pending) > self.LA:
            self._pv()

    def flush(self):
        while self.pending:
            self._pv()
        self._run_deferred(force=True)


def emit_causal_head(kb, st, nc, *, kT, qT, v1, kr=None, qr=None, fq3=None, bias_col=None, ones_bf=None,
                     ident_bf, cmask_bf, accs, acc_free, tr_bank, tr_free, osb, ostage, o_dst, op_waits,
                     act_waits, rinv):
    NQ = S // 512
    first = [True]
    for Q in range(NQ):
        aset = Q % 2
        q0 = Q * 512
        state = {"tks": []}
        for j in range(4 * Q + 4):
            r = j - 4 * Q
            c0 = 128 * r if r > 0 else 0
            diag = r >= 0

            def qk(bank, j=j, c0=c0, diag=diag, q0=q0):
                nc.tensor.matmul(bank[:, c0:512], lhsT=kT[:, j * 128:(j + 1) * 128], rhs=qT[:, q0 + c0:q0 + 512],
                                 start=True, stop=False)
                if diag:
                    nc.tensor.matmul(bank[:, c0:c0 + 128], lhsT=ident_bf[:, :], rhs=cmask_bf[:, :], start=False, stop=False)
                if kr is not None:
                    mm = nc.tensor.matmul(bank[:, c0:512], lhsT=kr[0:64, j * 128:(j + 1) * 128],
                                          rhs=qr[0:64, q0 + c0:q0 + 512], start=False, stop=True)
                else:
                    mm = nc.tensor.matmul(bank[:, c0:512], lhsT=ones_bf[:, :], rhs=fq3[:, q0 + c0:q0 + 512],
                                          start=False, stop=True)
                return mm

            def act(bank, pbuf, j=j, c0=c0):
                if bias_col is not None:
                    return nc.scalar.activation(out=pbuf[:, c0:512], in_=bank[:, c0:512], func=AF.Exp, bias=bias_col(j))
                return nc.scalar.activation(out=pbuf[:, c0:512], in_=bank[:, c0:512], func=AF.Exp)

            def pv(pbuf, j=j, c0=c0, Q=Q, aset=aset):
                mm = None
                for qs in range(c0 // 128, 4):
                    a = accs[aset][qs // 2]
                    off = (qs % 2) * 256
                    mm = nc.tensor.matmul(a[:, off:off + 129], lhsT=pbuf[:, qs * 128:(qs + 1) * 128], rhs=v1[:, j, 0:129],
                                          start=(j == 0 and qs % 2 == 0), stop=(j == 4 * Q + qs))
                return mm

            tile = {"qk": qk, "act": act, "pv": pv}
            if first[0]:
                tile["qk_waits"] = list(op_waits)
                tile["act_waits"] = list(act_waits)
                first[0] = False
            if j == 0:
                tile["pv_waits"] = (lambda aset=aset: [acc_free[aset]])
            if r == 3:
                def post(t_pv, Q=Q, aset=aset, q0=q0):
                    kb.wait("dve", t_pv, tr_free[0])
                    tks = []
                    for qs in range(4):
                        a = accs[aset][qs // 2]
                        off = (qs % 2) * 256
                        rc = rinv[:, aset * 4 + qs:aset * 4 + qs + 1]
                        t1 = kb.sig("dve", nc.vector.reciprocal(out=rc, in_=a[:, off + 128:off + 129]))
                        kb.wait("dve", t1)
                        t2 = kb.sig("dve", nc.vector.tensor_scalar(out=osb[:, qs * 128:(qs + 1) * 128], in0=a[:, off:off + 128],
                                                                   scalar1=rc, scalar2=None, op0=ALU.mult))
                        tks.append(t2)
                    acc_free[aset] = t2

                    def fin(tks=tks, q0=q0):
                        kb.wait("pe", *tks, tr_free[1])
                        for s4 in range(4):
                            tp = nc.tensor.transpose(out=tr_bank[:, s4 * 128:(s4 + 1) * 128],
                                                     in_=osb[:, s4 * 128:(s4 + 1) * 128], identity=ident_bf[:, :])
                        t3 = kb.sig("pe", tp)
                        tr_free[0] = t3
                        buf, t_prev = ostage.next()
                        kb.wait("act", t3, t_prev)
                        t4 = kb.sig("act", nc.scalar.copy(out=buf[:, :], in_=tr_bank[:, :]))
                        tr_free[1] = t4
                        ostage.store(o_dst[:, q0:q0 + 512], buf[:, :], t4)
                    st.defer(2, fin)
                tile["post"] = post
            st.push(tile)


DILS = (1, 4, 16)


def emit_dilated_head(kb, st, nc, *, kT, qT, vperm, bias_hi, bias_lo, hl, ones_bf, ident_bf, dbanks, slot_free,
                      accN, accZ, ostage, o_dst, op_waits, rz):
    first = [True]
    cnt = [0]
    for sb in range(S // 2048):
        for p, d in enumerate(DILS):
            nbpr = 64 // d
            for r in range(d):
                for n in range(sb * 16 // d, (sb + 1) * 16 // d):
                    base = r + d * 128 * n
                    QS = slice(base, base + 127 * d + 1, d)
                    PS = slice(base - 128 * d, base - d + 1, d)
                    ncol = 256 if n > 0 else 128
                    blk = r * nbpr + n
                    ph = p * 4 + hl
                    LS = slice(base - sb * 2048, base - sb * 2048 + 127 * d + 1, d)

                    def qk(bank, QS=QS, PS=PS, n=n, ncol=ncol, ph=ph):
                        nc.tensor.matmul(bank[:, 0:128], lhsT=kT[:, QS], rhs=qT[:, QS], start=True, stop=False)
                        if n > 0:
                            nc.tensor.matmul(bank[:, 128:256], lhsT=kT[:, PS], rhs=qT[:, QS], start=False, stop=False)
                        nc.tensor.matmul(bank[:, 0:ncol], lhsT=ident_bf[:, :], rhs=bias_hi[:, ph, 0:ncol], start=False, stop=False)
                        return nc.tensor.matmul(bank[:, 0:ncol], lhsT=ident_bf[:, :], rhs=bias_lo[:, ph, 0:ncol],
                                                start=False, stop=True)

                    def act(bank, pbuf, ncol=ncol):
                        return nc.scalar.activation(out=pbuf[:, 0:ncol], in_=bank[:, 0:ncol], func=AF.Exp)

                    slot = cnt[0] % 4
                    cnt[0] += 1
                    dbank = dbanks[slot]
                    off = 0

                    def pv(pbuf, p=p, blk=blk, n=n, dbank=dbank, off=off):
                        nc.tensor.matmul(dbank[:, off:off + 128], lhsT=vperm[p][:, blk, :], rhs=pbuf[:, 0:128],
                                         start=True, stop=(n == 0))
                        if n > 0:
                            nc.tensor.matmul(dbank[:, off:off + 128], lhsT=vperm[p][:, blk - 1, :], rhs=pbuf[:, 128:256],
                                             start=False, stop=True)
                        mm = nc.tensor.matmul(dbank[:, off + 128:off + 256], lhsT=ones_bf[:, :], rhs=pbuf[:, 0:128],
                                              start=False, stop=(n == 0))
                        if n > 0:
                            mm = nc.tensor.matmul(dbank[:, off + 128:off + 256], lhsT=ones_bf[:, :], rhs=pbuf[:, 128:256],
                                                  start=False, stop=True)
                        return mm

                    def post(t_pv, p=p, LS=LS, dbank=dbank, off=off, slot=slot):
                        kb.wait("dve", t_pv)
                        if p == 0:
                            nc.vector.tensor_copy(out=accN[:, LS], in_=dbank[:, off:off + 128])
                            tk = kb.sig("dve", nc.vector.tensor_copy(out=accZ[:, LS], in_=dbank[:, off + 128:off + 256]))
                        else:
                            nc.vector.tensor_tensor(out=accN[:, LS], in0=accN[:, LS], in1=dbank[:, off:off + 128], op=ALU.add)
                            tk = kb.sig("dve", nc.vector.tensor_tensor(out=accZ[:, LS], in0=accZ[:, LS],
                                                                       in1=dbank[:, off + 128:off + 256], op=ALU.add))
                        slot_free[slot] = tk

                    tile = {"qk": qk, "act": act, "pv": pv, "post": post,
                            "pv_waits": (lambda slot=slot: [slot_free[slot]])}
                    if first[0]:
                        tile["qk_waits"] = list(op_waits)
                        first[0] = False
                    st.push(tile)
        st.flush()
        last = [slot_free[i] for i in range(4)]
        kb.wait("dve", *last)
        t1 = kb.sig("dve", nc.vector.reciprocal(out=rz[:, :], in_=accZ[:, :]))
        buf, t_prev = ostage.next()
        kb.wait("dve", t1, t_prev)
        t2 = kb.sig("dve", nc.vector.tensor_tensor(out=buf[:, :], in0=accN[:, :], in1=rz[:, :], op=ALU.mult))
        ostage.store(o_dst[:, sb * 2048:(sb + 1) * 2048], buf[:, :], t2)
        kb.dil_last = t2


def attn_common(kb, nc, consts_in):
    c = {}
    c["ident_bf"] = kb.sbuf("ident_bf", [128, 128], BF16)
    c["cmask_bf"] = kb.sbuf("cmask_bf", [128, 128], BF16)
    c["ones_bf"] = kb.sbuf("ones_bf", [128, 128], BF16)
    c["U_f"] = kb.sbuf("U_f", [128, 128], F32)
    c["ones_f"] = kb.sbuf("ones_f", [128, 128], F32)
    d = DmaBuf(kb, "attc")
    d.dma("pool", c["ident_bf"][:], consts_in[0])
    d.dma("pool", c["cmask_bf"][:], consts_in[1])
    d.dma("pool", c["ones_bf"][:], consts_in[3])
    d.dma("sp", c["U_f"][:], consts_in[2])
    d.dma("sp", c["ones_f"][:], consts_in[3])
    c["ticket"] = d.ticket()
    sb = [kb.psum(f"sbank{i}", [128, 512]) for i in range(3)]
    c["accs"] = [[kb.psum(f"acc{a}{i}", [128, 512]) for i in range(2)] for a in range(2)]
    c["tr_bank"] = kb.psum("tr_bank", [128, 512], BF16)
    pb = [kb.sbuf(f"pbuf{i}", [128, 512], BF16) for i in range(4)]
    c["st"] = Stream(kb, sb, pb, LA=2)
    c["sbanks"] = sb
    c["osb"] = kb.sbuf("osb", [128, 512], BF16)
    c["rinv"] = kb.sbuf("rinv", [128, 8], F32)
    c["ostage"] = OutStage(kb, "ost", [128, 512], BF16, n=3)
    c["acc_free"] = [None, None]
    c["tr_free"] = [None, None]
    return c


def build_attn0(n_a=4, n_b=4):
    nc = bass.Bass("TRN2", target_bir_lowering=False)
    qkA = nc.dram_tensor("qkA", [2, 512, S], BF16, kind="ExternalInput").ap()
    vaP = nc.dram_tensor("vaP", [3, 4, 128, 64 * 128], BF16, kind="ExternalInput").ap()
    qkB = nc.dram_tensor("qkB", [2, 512, S], BF16, kind="ExternalInput").ap()
    vb = nc.dram_tensor("vb", [4, 128, 64 * 129], BF16, kind="ExternalInput").ap()
    fl = nc.dram_tensor("fl", [128, 256], F32, kind="ExternalInput").ap()
    bfr = nc.dram_tensor("bfr", [128, 256], F32, kind="ExternalInput").ap()
    biasm = nc.dram_tensor("biasm", [128, 12 * 256], F32, kind="ExternalInput").ap()
    consts_in = nc.dram_tensor("consts", [4, 128, 128], F32, kind="ExternalInput").ap()
    oT = nc.dram_tensor("oT", [1024, S], BF16, kind="ExternalOutput").ap()
    fscr = nc.dram_tensor("fscr", [3, 4, S], BF16).ap()
    kb = KB(nc)
    C = attn_common(kb, nc, consts_in)
    st = C["st"]
    qbuf = [kb.sbuf(f"qbuf{i}", [128, S], BF16) for i in range(2)]
    kbuf = [kb.sbuf(f"kbuf{i}", [128, S], BF16) for i in range(2)]
    qkl = [DmaBuf(kb, f"qkl{i}") for i in range(2)]
    vperm = [kb.sbuf(f"vperm{p}", [128, 64, 128], BF16) for p in range(3)]
    vpl = DmaBuf(kb, "vpl")
    v1 = [kb.sbuf("v1_0", [128, 64, 129], BF16)]
    fq3 = [kb.sbuf("fq3_0", [128, S], BF16)]
    fql = DmaBuf(kb, "fql")
    bl = [DmaBuf(kb, f"bl{i}") for i in range(2)]
    accN = kb.sbuf("accN", [128, 2048], F32)
    accZ = kb.sbuf("accZ", [128, 2048], F32)
    dstage = OutStage(kb, "dst", [128, 2048], BF16, n=1)
    bias32 = kb.sbuf("bias32", [128, 4 * 256], F32)
    rz = accZ
    bias_hi = kb.sbuf("bias_hi", [128, 12, 256], BF16)
    bias_lo = kb.sbuf("bias_lo", [128, 12, 256], BF16)
    fl_sb = kb.sbuf("fl_sb", [128, 256], F32)
    bf_sb = kb.sbuf("bf_sb", [128, 256], F32)
    lsb = kb.sbuf("lsb", [128, 256], F32)
    tot = kb.sbuf("tot", [128, 256], F32)
    offs = kb.sbuf("offs", [128, 256], F32)
    fneg = kb.sbuf("fneg", [128, 256], F32)
    r1 = kb.sbuf("r1", [128, 256], F32)
    parts = kb.sbuf("parts", [128, 3, 256], BF16)
    frT = kb.sbuf("frT", [64, 3, 512], BF16)

    sm = DmaBuf(kb, "small")
    sm.dma("sp", fl_sb[:], fl)
    sm.dma("sp", bf_sb[:], bfr)
    t_ms = None
    t_ms1 = None
    t_ms = kb.sig("pool", nc.gpsimd.memset(fq3[0][:, :], 0.0))
    t_ms2 = kb.sig("dve", nc.vector.memset(offs[:, :], 0.0))

    kb.wait("dve", sm.ticket())
    bh = bias_hi[:, :, :].rearrange("p a c -> p (a c)")
    blo = bias_lo[:, :, :].rearrange("p a c -> p (a c)")
    bld = DmaBuf(kb, "bias32")
    t_bias = None
    for ch in range(3):
        cs = slice(ch * 1024, (ch + 1) * 1024)
        kb.wait("sp", t_bias)
        bld.dma("sp", bias32[:, :], biasm[:, cs])
        kb.wait("dve", bld.ticket())
        t = kb.sig("dve", nc.vector.tensor_copy(out=bh[:, cs], in_=bias32[:, :]))
        kb.wait("dve", t)
        t = kb.sig("dve", nc.vector.tensor_tensor(out=bias32[:, :], in0=bias32[:, :], in1=bh[:, cs], op=ALU.subtract))
        kb.wait("dve", t)
        t_bias = kb.sig("dve", nc.vector.tensor_copy(out=blo[:, cs], in_=bias32[:, :]))

    if n_b > 0:
        t = kb.sig("dve", nc.vector.tensor_tensor(out=fl_sb[:, :], in0=fl_sb[:, :], in1=bf_sb[:, :], op=ALU.add))
        kb.wait("act", t)
        t = kb.sig("act", nc.scalar.activation(out=lsb[:, :], in_=fl_sb[:, :], func=AF.Exp, scale=-1.0))
        kb.wait("act", t)
        t_l = kb.sig("act", nc.scalar.activation(out=lsb[:, :], in_=lsb[:, :], func=AF.Ln, bias=1.0))
        kb.wait("pe", t_l, C["ticket"])
        b0, b1 = C["sbanks"][0], C["sbanks"][1]
        t_w = kb.sig("pe", nc.tensor.matmul(b0[:, 0:256], lhsT=C["U_f"][:, :], rhs=lsb[:, :], start=True, stop=True))
        t_t = kb.sig("pe", nc.tensor.matmul(b1[:, 0:256], lhsT=C["ones_f"][:, :], rhs=lsb[:, :], start=True, stop=True))
        kb.wait("dve", t_t, t_ms2)
        t = kb.sig("dve", nc.vector.tensor_copy(out=tot[:, :], in_=b1[:, 0:256]))
        o3 = offs[:, :].rearrange("p (h j) -> p h j", h=4)
        t3 = tot[:, :].rearrange("p (h j) -> p h j", h=4)
        for j in range(1, 64):
            kb.wait("dve", t)
            t = kb.sig("dve", nc.vector.tensor_tensor(out=o3[:, :, j], in0=o3[:, :, j - 1], in1=t3[:, :, j - 1], op=ALU.add))
        kb.wait("dve", t, t_w)
        t_f = kb.sig("dve", nc.vector.tensor_tensor(out=fneg[:, :], in0=offs[:, :], in1=b0[:, 0:256], op=ALU.add))
        st.s_free[0] = t_f
        st.s_free[1] = t
        kb.wait("dve", t_f)
        t = kb.sig("dve", nc.vector.tensor_scalar(out=parts[:, 0, :], in0=fneg[:, :], scalar1=-1.0, scalar2=None, op0=ALU.mult))
        kb.wait("dve", t)
        t = kb.sig("dve", nc.vector.scalar_tensor_tensor(out=r1[:, :], in0=fneg[:, :], scalar=-1.0, in1=parts[:, 0, :],
                                                          op0=ALU.mult, op1=ALU.subtract))
        kb.wait("dve", t)
        t = kb.sig("dve", nc.vector.tensor_copy(out=parts[:, 1, :], in_=r1[:, :]))
        kb.wait("dve", t)
        t = kb.sig("dve", nc.vector.tensor_tensor(out=r1[:, :], in0=r1[:, :], in1=parts[:, 1, :], op=ALU.subtract))
        kb.wait("dve", t)
        t_parts = kb.sig("dve", nc.vector.tensor_copy(out=parts[:, 2, :], in_=r1[:, :]))
        fs = DmaBuf(kb, "fscr")
        t_c = None
        for part in range(3):
            kb.wait("pe", t_parts, t_c)
            for h in range(4):
                tp = nc.tensor.transpose(out=C["tr_bank"][0:64, h * 128:(h + 1) * 128], in_=parts[:, part, h * 64:(h + 1) * 64],
                                         identity=C["ident_bf"][:, :])
            t_tp = kb.sig("pe", tp)
            kb.wait("act", t_tp)
            t_c = kb.sig("act", nc.scalar.copy(out=frT[:, part, :], in_=C["tr_bank"][0:64, :]))
            kb.wait("sp", t_c)
            fs.dma("sp", fscr[part].rearrange("h (j p) -> j h p", p=128), frT[:, part, :].rearrange("j (h p) -> j h p", p=128))
        C["tr_free"][1] = t_c
        t_fscr = fs.ticket()

    heads = [("A", h) for h in range(n_a)] + [("B", h) for h in range(n_b)]
    last_pv = {}

    def load_head(idx):
        kind, hl = heads[idx]
        b = idx % 2
        prev = last_pv.get(idx - 2)
        kb.wait("sp", prev)
        src = qkA if kind == "A" else qkB
        qkl[b].dma("sp", qbuf[b][:, :], src[0, hl * 128:(hl + 1) * 128, :])
        qkl[b].dma("sp", kbuf[b][:, :], src[1, hl * 128:(hl + 1) * 128, :])

    def load_fq3(idx):
        kind, hl = heads[idx]
        kb.wait("sp", last_pv.get(idx - 1), t_ms, t_ms1, t_fscr)
        fql.dma("sp", v1[0][:, :, :].rearrange("p j d -> p (j d)"), vb[hl])
        for part in range(3):
            fql.dma("sp", fq3[0][32 * part:32 * part + 1, :], fscr[part, hl:hl + 1, :])

    def load_vperm(idx):
        kind, hl = heads[idx]
        kb.wait("sp", last_pv.get(idx - 1))
        for p in range(3):
            vpl.dma("sp", vperm[p][:, :, :].rearrange("p b d -> p (b d)"), vaP[p, hl])

    load_head(0)
    for idx, (kind, hl) in enumerate(heads):
        b = idx % 2
        if kind == "A":
            load_vperm(idx)
        else:
            load_fq3(idx)
        if idx + 1 < len(heads):
            load_head(idx + 1)
        if kind == "A":
            emit_dilated_head(kb, st, nc, kT=kbuf[b], qT=qbuf[b], vperm=vperm, bias_hi=bias_hi, bias_lo=bias_lo, hl=hl,
                              ones_bf=C["ones_bf"], ident_bf=C["ident_bf"], dbanks=C["accs"][0] + C["accs"][1],
                              slot_free=kb.__dict__.setdefault("slot_free", [None] * 4), accN=accN, accZ=accZ,
                              ostage=dstage, o_dst=oT[hl * 128:(hl + 1) * 128, :],
                              op_waits=[qkl[b].ticket(), vpl.ticket(), C["ticket"], t_bias], rz=rz)
            C["acc_free"][0] = kb.dil_last
            C["acc_free"][1] = kb.dil_last
        else:
            emit_causal_head(kb, st, nc, kT=kbuf[b], qT=qbuf[b], v1=v1[0], fq3=fq3[0],
                             bias_col=(lambda j, hl=hl: fneg[:, hl * 64 + j:hl * 64 + j + 1]), ones_bf=C["ones_bf"],
                             ident_bf=C["ident_bf"], cmask_bf=C["cmask_bf"], accs=C["accs"], acc_free=C["acc_free"],
                             tr_bank=C["tr_bank"], tr_free=C["tr_free"], osb=C["osb"], ostage=C["ostage"],
                             o_dst=oT[512 + hl * 128:512 + (hl + 1) * 128, :],
                             op_waits=[qkl[b].ticket(), fql.ticket(), C["ticket"]], act_waits=[t_f],
                             rinv=C["rinv"])
            st.flush()
        last_pv[idx] = st.last_pv
    kb.wait("sp", *C["ostage"].all_tickets(), *dstage.all_tickets())
    kb.close()
    return nc


def t5_bucket_np(n):
    max_exact = 16
    nf = np.maximum(n, 1).astype(np.float32)
    large = max_exact + (np.log(nf / max_exact) / math.log(2048 / max_exact) * (32 - max_exact)).astype(np.int32)
    large = np.minimum(large, 31)
    return np.where(n < max_exact, n, large)


def attn_consts():
    ident = np.eye(128, dtype=np.float32)
    k = np.arange(128)[:, None]
    q = np.arange(128)[None, :]
    cmaskT = np.where(k <= q, 0.0, NEG).astype(np.float32)
    U = (k <= q).astype(np.float32)
    ones = np.ones((128, 128), np.float32)
    return np.stack([ident, cmaskT, U, ones])


def run_attn0(pre, rel_bias, b_forget, n_a=4, n_b=4):
    consts = attn_consts()
    dist = np.arange(129)
    kk = np.arange(128)[:, None]
    cc = np.arange(256)[None, :]
    dd = cc - kk
    valid = (dd >= 0) & (dd <= 128)
    in_maps = []
    for core in range(NCORES):
        b, g = core // 4, core % 4
        cores_b = [b * 4 + i for i in range(4)]
        qk = np.concatenate([pre[c]["qkT"] for c in cores_b], axis=1)
        v = np.concatenate([pre[c]["v"] for c in cores_b], axis=0)
        f = np.concatenate([pre[c]["fT"] for c in cores_b], axis=1)
        hs = slice(g * 512, (g + 1) * 512)
        qkA = np.stack([qk[0:2048][hs], qk[2048:4096][hs]])
        qkB = np.stack([qk[4096:6144][hs], qk[6144:8192][hs]])
        va = v[:, 0:2048][:, hs]
        vbb = v[:, 2048:4096][:, hs]
        vaP = np.stack([va.reshape(S // d, d, 512).transpose(1, 0, 2).reshape(64, 128, 4, 128).transpose(2, 1, 0, 3)
                        .reshape(4, 128, 64 * 128) for d in DILS])
        vb1 = np.ones((4, 128, 64, 129), NPBF)
        vb1[:, :, :, 0:128] = vbb.reshape(64, 128, 4, 128).transpose(2, 1, 0, 3)
        vbb = vb1.reshape(4, 128, 64 * 129)
        fl = f[g * 4:(g + 1) * 4].reshape(4, 64, 128).transpose(2, 0, 1).reshape(128, 256)
        bfr = np.broadcast_to(np.repeat(b_forget[g * 4:(g + 1) * 4], 64)[None, :], (128, 256))
        biasm = np.full((128, 12, 256), NEG, np.float32)
        for p, d in enumerate(DILS):
            bucket = t5_bucket_np(np.maximum(dd, 0) * d)
            for hl in range(4):
                vals = rel_bias[:, g * 4 + hl][bucket]
                biasm[:, p * 4 + hl, :] = np.where(valid, vals, NEG)
        in_maps.append({"qkA": np.ascontiguousarray(qkA), "vaP": np.ascontiguousarray(vaP),
                        "qkB": np.ascontiguousarray(qkB), "vb": np.ascontiguousarray(vbb),
                        "fl": np.ascontiguousarray(fl), "bfr": np.ascontiguousarray(bfr),
                        "biasm": np.ascontiguousarray(biasm.reshape(128, 12 * 256)), "consts": consts})
    nc = build_attn0(n_a, n_b)
    res = _launch(nc, in_maps)
    return res.results


def alloc_post(kb, tag, TH, nld=2):
    P = {}
    P["ol"] = [kb.sbuf(f"{tag}_o{i}", [128, TH], BF16) for i in range(nld)]
    P["zl"] = [kb.sbuf(f"{tag}_z{i}", [128, TH], BF16) for i in range(nld)]
    P["ld"] = [DmaBuf(kb, f"{tag}_oz{i}") for i in range(nld)]
    P["free"] = [None, None]
    P["xl"] = [kb.sbuf(f"{tag}_x{i}", [128, TH], F32) for i in range(2)]
    P["xld"] = [DmaBuf(kb, f"{tag}_x{i}") for i in range(2)]
    P["xfree"] = [None, None]
    P["stage"] = OutStage(kb, f"{tag}_st", [128, TH], F32)
    P["gbuf_free"] = None
    return P


def emit_post(kb, nc, P, *, oT, zT, xT, w_out, gate, x_out, gbuf, wbufs, banks, tag, toff, TH, evac_extra=None,
              bank_ids=None):
    ol, zl, ld, free = P["ol"], P["zl"], P["ld"], P["free"]
    tsl = slice(toff, toff + TH)
    g_ready = []
    for kc in range(KC):
        i = kc % len(ol)
        kb.wait("sp", free[i])
        ld[i].dma("sp", ol[i][:, :], oT[kc * 128:(kc + 1) * 128, tsl])
        ld[i].dma("sp", zl[i][:, :], zT[kc * 128:(kc + 1) * 128, tsl])
        e = "dve" if kc % 2 == 0 else "pool"
        kb.wait(e, ld[i].ticket(), P["gbuf_free"])
        eng = nc.vector if e == "dve" else nc.gpsimd
        free[i] = kb.sig(e, eng.tensor_tensor(out=gbuf[:, kc, :], in0=ol[i][:, :], in1=zl[i][:, :], op=ALU.mult))
        g_ready.append(free[i])
    xl, xld, stage = P["xl"], P["xld"], P["stage"]
    TT = TH // 512
    state = {}

    def evac(c, tt, bank, t_mm):
        i = c % 2
        if tt == 0:
            buf, t_prev = stage.next()
            state["buf"], state["prev"], state["tks"] = buf, t_prev, []
            kb.wait("sp", P["xfree"][i])
            xld[i].dma("sp", xl[i][:, :], xT[c * 128:(c + 1) * 128, tsl])
        sl = slice(tt * 512, (tt + 1) * 512)
        kb.wait("dve", t_mm, state["prev"], xld[i].ticket())
        tk = kb.sig("dve", nc.vector.scalar_tensor_tensor(out=state["buf"][:, sl], in0=bank[:, :], scalar=gate[:, c:c + 1],
                                                          in1=xl[i][:, sl], op0=ALU.mult, op1=ALU.add))
        state["tks"].append(tk)
        if evac_extra is not None:
            evac_extra(c, tt, state["buf"][:, sl], tk)
        if tt == TT - 1:
            P["xfree"][i] = tk
            if x_out is not None:
                stage.store(x_out[c * 128:(c + 1) * 128, tsl], state["buf"][:, :], *state["tks"])
        return tk

    kb.wait("pe", *g_ready[-2:])
    emit_linear_fm(kb, w_out, KC, gbuf, KC, banks, evac, tag + "_fm", wbufs=wbufs, bank_ids=bank_ids)
    P["gbuf_free"] = kb.bank_free_last_mm
    return stage.all_tickets()


TWO_PI = 2.0 * math.pi
C1_2PI = 6.28125
C2_2PI = TWO_PI - C1_2PI
QSC1 = 1.0 / math.sqrt(192.0)


def emit_rope_tables(kb, nc, pos_i, invf, cos_t, sin_t, tmp):
    ang, a, b = tmp
    V = nc.vector

    def step(instr, *w):
        return kb.sig("dve", instr)

    t = kb.sig("dve", V.tensor_copy(out=ang[:, :], in_=pos_i[:, :]))
    kb.wait("dve", t)
    t = kb.sig("dve", V.tensor_scalar(out=ang[:, :], in0=ang[:, :], scalar1=invf[:, 0:1], scalar2=None, op0=ALU.mult))
    kb.wait("dve", t)
    t = kb.sig("dve", V.tensor_scalar(out=a[:, :], in0=ang[:, :], scalar1=1.0 / TWO_PI, scalar2=None, op0=ALU.mult))
    kb.wait("dve", t)
    t = kb.sig("dve", V.tensor_copy(out=pos_i[:, :], in_=a[:, :]))
    kb.wait("dve", t)
    t = kb.sig("dve", V.tensor_copy(out=a[:, :], in_=pos_i[:, :]))
    kb.wait("dve", t)
    t = kb.sig("dve", V.scalar_tensor_tensor(out=b[:, :], in0=a[:, :], scalar=-C1_2PI, in1=ang[:, :], op0=ALU.mult, op1=ALU.add))
    kb.wait("dve", t)
    t = kb.sig("dve", V.scalar_tensor_tensor(out=b[:, :], in0=a[:, :], scalar=-C2_2PI, in1=b[:, :], op0=ALU.mult, op1=ALU.add))
    kb.wait("dve", t)
    t = kb.sig("dve", V.tensor_single_scalar(out=a[:, :], in_=b[:, :], scalar=math.pi, op=ALU.is_gt))
    kb.wait("dve", t)
    t = kb.sig("dve", V.scalar_tensor_tensor(out=b[:, :], in0=a[:, :], scalar=-TWO_PI, in1=b[:, :], op0=ALU.mult, op1=ALU.add))
    kb.wait("dve", t)
    t = kb.sig("dve", V.tensor_scalar(out=b[:, :], in0=b[:, :], scalar1=-math.pi, scalar2=math.pi, op0=ALU.max, op1=ALU.min))
    kb.wait("act", t)
    t_sin = kb.sig("act", nc.scalar.activation(out=sin_t[:, :], in_=b[:, :], func=AF.Sin))
    kb.wait("dve", t)
    t = kb.sig("dve", V.tensor_scalar(out=a[:, :], in0=b[:, :], scalar1=math.pi / 2, scalar2=None, op0=ALU.add))
    kb.wait("dve", t)
    t = kb.sig("dve", V.tensor_single_scalar(out=ang[:, :], in_=a[:, :], scalar=math.pi, op=ALU.is_gt))
    kb.wait("dve", t)
    t = kb.sig("dve", V.scalar_tensor_tensor(out=a[:, :], in0=ang[:, :], scalar=-TWO_PI, in1=a[:, :], op0=ALU.mult, op1=ALU.add))
    kb.wait("dve", t)
    t = kb.sig("dve", V.tensor_scalar(out=a[:, :], in0=a[:, :], scalar1=-math.pi, scalar2=math.pi, op0=ALU.max, op1=ALU.min))
    kb.wait("act", t)
    t_cos = kb.sig("act", nc.scalar.activation(out=cos_t[:, :], in_=a[:, :], func=AF.Sin))
    return t_sin, t_cos


def alloc_pre1(kb, TH):
    B = {}
    B["cqg"] = kb.sbuf("cqg", [128, 8, TH], BF16)
    B["ckvg"] = kb.sbuf("ckvg", [128, 4, TH], BF16)
    B["rq"] = kb.sbuf("rq", [128, TH], F32)
    B["rkv"] = kb.sbuf("rkv", [128, TH], F32)
    B["rkvc"] = kb.sbuf("rkvc", [128, TH // 128], F32)
    B["cos"] = kb.sbuf("cos_t", [128, TH], F32)
    B["sin"] = kb.sbuf("sin_t", [128, TH], F32)
    B["sqb"] = [kb.sbuf(f"sqb{i}", [128, 512], BF16) for i in range(2)]
    B["sq_free"] = [None, None]
    B["stage"] = OutStage(kb, "p1st", [128, TH], BF16)
    B["sd"] = kb.sbuf("p1sd", [128, 512], F32)
    B["kr"] = kb.sbuf("kr", [32, 4, 512], F32)
    B["kro"] = OutStage(kb, "kro", [32, 2, TH], BF16, n=1)
    B["krw"] = [kb.sbuf(f"krw{i}", [128, KC * 32], BF16) for i in range(2)]
    B["crs"] = kb.sbuf("crs", [128, 2, TH], F32)
    B["rtmp"] = [kb.sbuf(f"rtmp{i}", [128, 512], F32) for i in range(2)]
    B["rst"] = OutStage(kb, "rst", [128, 2, TH], BF16, n=1)
    B["vstage"] = OutStage(kb, "p1vst", [128, TH // 128, 256], BF16, n=1)
    return B


def emit_pre1(kb, nc, B, *, x1T, hT, A1, B1, banks, wbufs, ones_bf, ones_f, gq, gkv, invf, pos_dram, W, OUT, toff, TH, x_ready):
    TT = TH // 512
    tsl_all = slice(toff, toff + TH)
    tg = f"p1_{toff}"
    V = nc.vector
    cqg, ckvg, rq, rkv, rkvc, cos_t, sin_t, sqb, stage, sd = (B[k] for k in
                                                              ("cqg", "ckvg", "rq", "rkv", "rkvc", "cos", "sin", "sqb", "stage", "sd"))
    kb.wait("sp", *x_ready)
    with contextlib.ExitStack() as sub:
        kb_es, kb.es = kb.es, sub
        h_ready = emit_norm(kb, x1T[:, tsl_all], hT, ones_bf, A1, B1, banks[0], tg + "n")
        kb.es = kb_es
    kb.bank_free[0] = kb.norm_bank_ticket
    with contextlib.ExitStack() as sub:
        kb_es, kb.es = kb.es, sub
        pos_i = kb.sbuf(tg + "posi", [128, TH], I32)
        tmp = [kb.sbuf(tg + f"rt{i}", [128, TH], F32) for i in range(3)]
        pl = DmaBuf(kb, tg + "pos")
        kb.wait("sp", h_ready[-1])
        pl.dma("sp", pos_i[:, :], pos_dram[:, tsl_all])
        kb.wait("dve", pl.ticket(), h_ready[-1])
        t_sin, t_cos = emit_rope_tables(kb, nc, pos_i, invf, cos_t, sin_t, tmp)
        kb.es = kb_es
    st = {"pend": [], "sqn": 0}

    def flush_pend():
        for fn in st["pend"]:
            fn()
        st["pend"] = []

    def evac_lat(c, tt, bank, t_mm):
        flush_pend()
        isq = c < 8
        dst = cqg if isq else ckvg
        cc = c if isq else c - 8
        g = gq if isq else gkv
        sl = slice(tt * 512, (tt + 1) * 512)
        i = st["sqn"] % 2
        st["sqn"] += 1
        kb.wait("act", t_mm, B["sq_free"][i])
        kb.sig("act", nc.scalar.activation(out=dst[:, cc, sl], in_=bank[:, :], func=AF.Copy, scale=g[:, cc:cc + 1]))
        tk = kb.sig("act", nc.scalar.activation(out=sqb[i][:, :], in_=bank[:, :], func=AF.Square))
        sbi = (4 if isq else 6) + tt
        first, last = (cc == 0), (cc == (7 if isq else 3))

        def mm(i=i, tk=tk, sbi=sbi, first=first, last=last):
            kb.wait("pe", tk)
            if first:
                kb.wait("pe", kb.bank_free[sbi])
            t2 = kb.sig("pe", nc.tensor.matmul(banks[sbi][:, :], lhsT=ones_bf[:, :], rhs=sqb[i][:, :], start=first, stop=last))
            B["sq_free"][i] = t2
            st["last_ss"] = t2
        st["pend"].append(mm)
        return tk

    emit_linear_fm(kb, W["w1"], 12, hT, KC, banks, evac_lat, tg + "lat", h_ready=h_ready, wbufs=wbufs, bank_ids=[0, 1, 2, 3])
    flush_pend()
    t_r = None
    for which, (rt, nfeat, b0) in enumerate(((rq, 1024.0, 4), (rkv, 512.0, 6))):
        for tt in range(TT):
            kb.wait("act", st["last_ss"], t_r)
            t = kb.sig("act", nc.scalar.activation(out=sd[:, :], in_=banks[b0 + tt][:, :], func=AF.Sqrt, scale=1.0 / nfeat, bias=EPS))
            kb.bank_free[b0 + tt] = t
            kb.wait("dve", t)
            t_r = kb.sig("dve", nc.vector.reciprocal(out=rt[:, tt * 512:(tt + 1) * 512], in_=sd[:, :]))
    kb.wait("pe", t_r, kb.bank_free[4])
    for t8 in range(TH // 128):
        mm = nc.tensor.matmul(banks[4][:, t8:t8 + 1], lhsT=rkv[0:1, t8 * 128:(t8 + 1) * 128], rhs=ones_f[0:1, 0:1],
                              start=(t8 == 0), stop=True)
    t = kb.sig("pe", mm)
    kb.wait("dve", t)
    t_rc = kb.sig("dve", nc.vector.tensor_copy(out=rkvc[:, :], in_=banks[4][:, 0:TH // 128]))
    kb.bank_free[4] = t_rc

    kr, kro = B["kr"], B["kro"]
    ks = {}

    def evac_kr(c, tt, bank, t_mm):
        sl = slice(tt * 512, (tt + 1) * 512)
        if c == 0:
            kb.wait("dve", t_mm)
            tk = kb.sig("dve", V.tensor_copy(out=kr[:, tt, :], in_=bank[0:32, :]))
            ks[tt] = tk
            return tk
        if tt == 0:
            buf, t_prev = kro.next()
            ks["buf"], ks["prev"], ks["tks"] = buf, t_prev, []
        buf = ks["buf"]
        x1 = kr[:, tt, :]
        ta, tb = kr[:, 2, :], kr[:, 3, :]
        x2 = bank[0:32, :]
        c32, s32 = cos_t[0:32, sl], sin_t[0:32, sl]
        kb.wait("dve", t_mm, ks[tt], t_sin, t_cos, ks["prev"])
        t = kb.sig("dve", V.tensor_tensor(out=ta, in0=x1, in1=c32, op=ALU.mult))
        t = kb.sig("dve", V.tensor_tensor(out=tb, in0=x2, in1=s32, op=ALU.mult))
        kb.wait("dve", t)
        t = kb.sig("dve", V.tensor_tensor(out=buf[:, 0, sl], in0=ta, in1=tb, op=ALU.subtract))
        kb.wait("dve", t)
        t = kb.sig("dve", V.tensor_tensor(out=ta, in0=x1, in1=s32, op=ALU.mult))
        t = kb.sig("dve", V.tensor_tensor(out=tb, in0=x2, in1=c32, op=ALU.mult))
        kb.wait("dve", t)
        tk = kb.sig("dve", V.tensor_tensor(out=buf[:, 1, sl], in0=ta, in1=tb, op=ALU.add))
        ks["tks"].append(tk)
        if tt == TT - 1:
            kro.store(OUT["krT"][:, tsl_all].rearrange("(h p) t -> p h t", p=32), buf[:, :, :], *ks["tks"])
        return tk

    emit_linear_fm(kb, W["w_kr"], 2, hT, KC, banks, evac_kr, tg + "kr", mrows=32, wbufs=B["krw"], bank_ids=[0, 1, 2, 3])

    def staged(dst_dram, fn):
        s2 = {}

        def evac(c, tt, bank, t_mm):
            if tt == 0:
                buf, t_prev = stage.next()
                s2["buf"], s2["prev"], s2["tks"] = buf, t_prev, []
            sl = slice(tt * 512, (tt + 1) * 512)
            tk = fn(c, tt, bank, t_mm, s2["buf"][:, sl], s2["prev"], sl)
            s2["tks"].append(tk)
            if tt == TT - 1:
                stage.store(dst_dram[c * 128:(c + 1) * 128, tsl_all], s2["buf"][:, :], *s2["tks"])
            return tk
        return evac

    def f_z(c, tt, bank, t_mm, out, prev, sl):
        kb.wait("act", t_mm, prev)
        return kb.sig("act", nc.scalar.activation(out=out, in_=bank[:, :], func=AF.Silu))
    emit_linear_fm(kb, W["w_z"], 32, hT, KC, banks, staged(OUT["zT"], f_z), tg + "z", wbufs=wbufs, bank_ids=[0, 1, 2, 3])

    def f_qn(c, tt, bank, t_mm, out, prev, sl):
        kb.wait("dve", t_mm, prev, t_r)
        return kb.sig("dve", V.scalar_tensor_tensor(out=out, in0=bank[:, :], scalar=QSC1, in1=rq[:, sl],
                                                    op0=ALU.mult, op1=ALU.mult))
    emit_linear_fm(kb, W["w_qn"], 32, cqg, 8, banks, staged(OUT["qnT"], f_qn), tg + "qn", wbufs=wbufs, bank_ids=[0, 1, 2, 3])

    crs, rtmp, rst = B["crs"], B["rtmp"], B["rst"]
    kb.wait("dve", t_cos, t_sin, t_r, getattr(kb, "crs_free", None))
    kb.sig("dve", V.scalar_tensor_tensor(out=crs[:, 0, :], in0=cos_t[:, :], scalar=QSC1, in1=rq[:, :], op0=ALU.mult, op1=ALU.mult))
    t_crs = kb.sig("dve", V.scalar_tensor_tensor(out=crs[:, 1, :], in0=sin_t[:, :], scalar=QSC1, in1=rq[:, :], op0=ALU.mult, op1=ALU.mult))
    rs = {}

    def evac_qr(c, tt, bank, t_mm):
        sl = slice(tt * 512, (tt + 1) * 512)
        if c % 2 == 0:
            rs[("r1", tt)] = (bank, t_mm)
            return None
        b1, t1 = rs[("r1", tt)]
        if tt == 0:
            buf, t_prev = rst.next()
            rs["buf"], rs["prev"], rs["tks"] = buf, t_prev, []
        buf = rs["buf"]
        kb.wait("dve", t1, t_mm, rs["prev"], t_crs)
        a, b = rtmp
        t = kb.sig("dve", V.tensor_tensor(out=a[:, :], in0=b1[:, :], in1=crs[:, 0, sl], op=ALU.mult))
        t = kb.sig("dve", V.tensor_tensor(out=b[:, :], in0=bank[:, :], in1=crs[:, 1, sl], op=ALU.mult))
        kb.wait("dve", t)
        t = kb.sig("dve", V.tensor_tensor(out=buf[:, 0, sl], in0=a[:, :], in1=b[:, :], op=ALU.subtract))
        kb.wait("dve", t)
        t = kb.sig("dve", V.tensor_tensor(out=a[:, :], in0=b1[:, :], in1=crs[:, 1, sl], op=ALU.mult))
        t = kb.sig("dve", V.tensor_tensor(out=b[:, :], in0=bank[:, :], in1=crs[:, 0, sl], op=ALU.mult))
        kb.wait("dve", t)
        tk = kb.sig("dve", V.tensor_tensor(out=buf[:, 1, sl], in0=a[:, :], in1=b[:, :], op=ALU.add))
        rs["tks"].append(tk)
        kb.bank_free[((c - 1) * TT + tt) % 4] = tk
        kb.crs_free = tk
        if tt == TT - 1:
            g = c // 2
            dst = OUT["qrT"][g * 256:(g + 1) * 256, tsl_all].rearrange("(h p) t -> p h t", p=128)
            rst.store(dst, buf[:, :, :], *rs["tks"])
        return tk

    emit_linear_fm(kb, W["w_qr"], 16, cqg, 8, banks, evac_qr, tg + "qr", wbufs=wbufs, bank_ids=[0, 1, 2, 3])

    def f_kn(c, tt, bank, t_mm, out, prev, sl):
        kb.wait("dve", t_mm, prev, t_r)
        return kb.sig("dve", V.tensor_tensor(out=out, in0=bank[:, :], in1=rkv[:, sl], op=ALU.mult))
    emit_linear_fm(kb, W["w_kn"], 32, ckvg, 4, banks, staged(OUT["knT"], f_kn), tg + "kn", wbufs=wbufs, bank_ids=[0, 1, 2, 3])

    vstage = B["vstage"]
    vs = {}
    NTC = TH // 128

    def evac_v(j, t, bank, t_mm):
        if t == 0:
            buf, t_prev = vstage.next()
            vs["buf"], vs["prev"], vs["tks"] = buf, t_prev, []
        kb.wait("act", t_mm, vs["prev"], t_rc)
        tk = kb.sig("act", nc.scalar.activation(out=vs["buf"][:, t, :], in_=bank[:, 0:256], func=AF.Copy, scale=rkvc[:, t:t + 1]))
        vs["tks"].append(tk)
        if t == NTC - 1:
            dst = OUT["v"][toff:toff + TH, j * 256:(j + 1) * 256].rearrange("(t p) c -> p t c", p=128)
            vstage.store(dst, vs["buf"][:, :, :], *vs["tks"])
        return tk

    emit_linear_tm(kb, W["w_v"], 16, ckvg, 4, 256, banks, evac_v, tg + "v", wbufs, bank_ids=[0, 1, 2, 3])
    return B["kro"].all_tickets() + stage.all_tickets() + rst.all_tickets() + vstage.all_tickets()


TH = 1024


def build_mid():
    nc = bass.Bass("TRN2", target_bir_lowering=False)
    dt = nc.dram_tensor
    oT0 = dt("oT0", [D, T], BF16, kind="ExternalInput").ap()
    zT0 = dt("zT0", [D, T], BF16, kind="ExternalInput").ap()
    xT = dt("xT", [D, T], F32, kind="ExternalInput").ap()
    vecs = dt("vecs", [128, 96 + 96 + 32 + 8 + 4 + 1], F32, kind="ExternalInput").ap()
    pos = dt("pos", [128, T], I32, kind="ExternalInput").ap()
    ones_in = dt("ones", [128, 128], F32, kind="ExternalInput").ap()
    w_out0 = dt("w_out0", [KC, 128, KC * 128], F32, kind="ExternalInput").ap()
    W = {"w1": dt("w1", [12, 128, KC * 128], F32, kind="ExternalInput").ap(),
         "w_kr": dt("w_kr", [2, 128, KC * 32], F32, kind="ExternalInput").ap(),
         "w_z": dt("w_z", [32, 128, KC * 128], F32, kind="ExternalInput").ap(),
         "w_qn": dt("w_qn", [32, 128, 8 * 128], F32, kind="ExternalInput").ap(),
         "w_qr": dt("w_qr", [16, 128, 8 * 128], F32, kind="ExternalInput").ap(),
         "w_kn": dt("w_kn", [32, 128, 4 * 128], F32, kind="ExternalInput").ap(),
         "w_v": dt("w_v", [16, 128, 4 * 256], F32, kind="ExternalInput").ap()}
    x1T = dt("x1T", [D, T], F32, kind="ExternalOutput").ap()
    OUT = {"zT": dt("zT1", [D, T], BF16, kind="ExternalOutput").ap(),
           "qnT": dt("qnT", [D, T], BF16, kind="ExternalOutput").ap(),
           "qrT": dt("qrT", [2048, T], BF16, kind="ExternalOutput").ap(),
           "knT": dt("knT", [D, T], BF16, kind="ExternalOutput").ap(),
           "krT": dt("krT", [64, T], BF16, kind="ExternalOutput").ap(),
           "v": dt("v1", [T, D], BF16, kind="ExternalOutput").ap()}
    kb = KB(nc)
    banks = [kb.psum(f"bank{i}", [128, 512]) for i in range(8)]
    vsb = kb.sbuf("vsb", [128, 237], F32)
    ones_bf = kb.sbuf("ones_bf", [128, 128], BF16)
    ones_f = kb.sbuf("ones_f", [128, 128], F32)
    A1 = kb.sbuf("A1", [128, KC], F32)
    t_c = load_consts(kb, [(vsb[:], vecs), (ones_f[:], ones_in)])
    dc = DmaBuf(kb, "ones")
    dc.dma("pool", ones_bf[:], ones_in)
    mod0, mod1, g1 = vsb[:, 0:96], vsb[:, 96:192], vsb[:, 192:224]
    gq, gkv, invf = vsb[:, 224:232], vsb[:, 232:236], vsb[:, 236:237]
    kb.wait("dve", t_c)
    t_A = kb.sig("dve", nc.vector.scalar_tensor_tensor(out=A1[:], in0=mod1[:, 32:64], scalar=1.0, in1=g1, op0=ALU.add, op1=ALU.mult))
    kb.wait("act", t_c, t_A)
    kb.wait("pe", dc.ticket(), t_c)
    kb.wait("pool", t_c)
    wbufs = [kb.sbuf(f"wb{i}", [128, KC * 128], BF16) for i in range(2)]
    with contextlib.ExitStack() as sub:
        kb_es, kb.es = kb.es, sub
        gbuf = kb.sbuf("gbufA", [128, KC, T], BF16)
        P = alloc_post(kb, "pa", T, nld=1)
        kb.es = kb_es
        x_ready = emit_post(kb, nc, P, oT=oT0, zT=zT0, xT=xT, w_out=w_out0, gate=mod0[:, 64:96], x_out=x1T, gbuf=gbuf,
                            wbufs=wbufs, banks=banks, tag="pa", toff=0, TH=T)
    kb.barrier(*x_ready)
    hT = kb.sbuf("hT", [128, KC, TH], BF16)
    B = alloc_pre1(kb, TH)
    tks = []
    for half in range(T // TH):
        tks += emit_pre1(kb, nc, B, x1T=x1T, hT=hT, A1=A1, B1=mod1[:, 0:32], banks=banks, wbufs=wbufs, ones_bf=ones_bf,
                         ones_f=ones_f, gq=gq, gkv=gkv, invf=invf, pos_dram=pos, W=W, OUT=OUT, toff=half * TH, TH=TH,
                         x_ready=x_ready)
    kb.wait("sp", *x_ready, *tks)
    kb.close()
    return nc


def mid_weights(w_out0, w_in_odd, w_q_b, w_kv_b):
    Wd = {"w_out0": lay_fm(w_out0, 128)}
    Wd["w1"] = lay_fm(w_in_odd[:, 0:1536], 128)
    Wd["w_kr"] = lay_fm(w_in_odd[:, 1536:1600], 32)
    Wd["w_z"] = lay_fm(w_in_odd[:, 1600:5696], 128)
    wq = w_q_b.reshape(1024, 32, 192)
    Wd["w_qn"] = lay_fm(np.ascontiguousarray(wq[:, :, 0:128]).reshape(1024, 4096), 128)
    r1 = wq[:, :, 128:160].reshape(1024, 8, 128)
    r2 = wq[:, :, 160:192].reshape(1024, 8, 128)
    Wd["w_qr"] = lay_fm(np.stack([r1, r2], axis=2).reshape(1024, 2048), 128)
    wkv = w_kv_b.reshape(512, 32, 256)
    Wd["w_kn"] = lay_fm(np.ascontiguousarray(wkv[:, :, 0:128]).reshape(512, 4096), 128)
    Wd["w_v"] = lay_fm(np.ascontiguousarray(wkv[:, :, 128:256]).reshape(512, 4096), 256)
    return Wd


def rope_invf():
    half = 32
    inv = (np.float32(10000.0) ** (-(np.arange(half, dtype=np.float32) / np.float32(half)))).astype(np.float32)
    return np.tile(inv, 4).reshape(128, 1)


def run_mid(oT_list, zT_list, xT_list, mod, g1, gq, gkv, positions, Wd):
    ones = np.ones((128, 128), np.float32)
    invf = rope_invf()
    in_maps = []
    for core in range(NCORES):
        b, q = core // 4, core % 4
        vecs = np.concatenate([colT(mod[0][b]), colT(mod[1][b]), colT(g1), colT(gq), colT(gkv), invf], axis=1)
        pos = np.ascontiguousarray(np.broadcast_to(positions[b, q * T:(q + 1) * T][None, :], (128, T))).astype(np.int32)
        m = {"oT0": oT_list[core], "zT0": zT_list[core], "xT": xT_list[core], "vecs": np.ascontiguousarray(vecs),
             "pos": pos, "ones": ones}
        m.update(Wd)
        in_maps.append(m)
    nc = build_mid()
    res = _launch(nc, in_maps)
    return res.results


def build_attn1(n_h=8):
    nc = bass.Bass("TRN2", target_bir_lowering=False)
    qn = nc.dram_tensor("qn", [8, 128, S], BF16, kind="ExternalInput").ap()
    qr = nc.dram_tensor("qr", [8, 64, S], BF16, kind="ExternalInput").ap()
    kn = nc.dram_tensor("kn", [8, 128, S], BF16, kind="ExternalInput").ap()
    kr = nc.dram_tensor("kr", [64, S], BF16, kind="ExternalInput").ap()
    vv = nc.dram_tensor("vv", [8, 128, 64 * 129], BF16, kind="ExternalInput").ap()
    consts_in = nc.dram_tensor("consts", [4, 128, 128], F32, kind="ExternalInput").ap()
    oT = nc.dram_tensor("oT", [1024, S], BF16, kind="ExternalOutput").ap()
    kb = KB(nc)
    C = attn_common(kb, nc, consts_in)
    st = C["st"]
    qnb = [kb.sbuf(f"qnb{i}", [128, S], BF16) for i in range(2)]
    knb = [kb.sbuf(f"knb{i}", [128, S], BF16) for i in range(2)]
    qrb = [kb.sbuf(f"qrb{i}", [64, S], BF16) for i in range(2)]
    v1 = [kb.sbuf(f"v1_{i}", [128, 64, 129], BF16) for i in range(2)]
    krb = kb.sbuf("krb", [64, S], BF16)
    ld = [DmaBuf(kb, f"ld{i}") for i in range(2)]
    kl = DmaBuf(kb, "kl")
    kl.dma("sp", krb[:, :], kr)
    last_pv = {}

    def load_head(h):
        b = h % 2
        kb.wait("sp", last_pv.get(h - 2))
        ld[b].dma("sp", qnb[b][:, :], qn[h])
        ld[b].dma("sp", knb[b][:, :], kn[h])
        ld[b].dma("sp", qrb[b][:, :], qr[h])
        ld[b].dma("sp", v1[b][:, :, :].rearrange("p j d -> p (j d)"), vv[h])

    load_head(0)
    for h in range(n_h):
        b = h % 2
        if h + 1 < n_h:
            load_head(h + 1)
        emit_causal_head(kb, st, nc, kT=knb[b], qT=qnb[b], v1=v1[b], kr=krb, qr=qrb[b], ident_bf=C["ident_bf"],
                         cmask_bf=C["cmask_bf"], accs=C["accs"], acc_free=C["acc_free"], tr_bank=C["tr_bank"],
                         tr_free=C["tr_free"], osb=C["osb"], ostage=C["ostage"], o_dst=oT[h * 128:(h + 1) * 128, :],
                         op_waits=[ld[b].ticket(), kl.ticket(), C["ticket"]], act_waits=[], rinv=C["rinv"])
        st.flush()
        last_pv[h] = st.last_pv
    kb.wait("sp", *C["ostage"].all_tickets())
    kb.close()
    return nc


def gather_tok(res, key, b):
    return np.concatenate([res[b * 4 + i][key] for i in range(4)], axis=1)


def run_attn1(mid, n_h=8):
    consts = attn_consts()
    in_maps = []
    per_b = {}
    for b in range(NB):
        qnT = gather_tok(mid, "qnT", b)
        qrT = gather_tok(mid, "qrT", b)
        knT = gather_tok(mid, "knT", b)
        krT = gather_tok(mid, "krT", b)
        v = np.concatenate([mid[b * 4 + i]["v1"] for i in range(4)], axis=0)
        qr_h = qrT.reshape(8, 2, 4, 32, S).transpose(0, 2, 1, 3, 4).reshape(32, 64, S)
        per_b[b] = (qnT.reshape(32, 128, S), qr_h, knT.reshape(32, 128, S), krT, v)
    for core in range(NCORES):
        b, g = core // 4, core % 4
        qn_, qr_, kn_, kr_, v = per_b[b]
        hs = slice(g * 8, (g + 1) * 8)
        v1 = np.ones((8, 128, 64, 129), NPBF)
        v1[:, :, :, 0:128] = v[:, g * 1024:(g + 1) * 1024].reshape(64, 128, 8, 128).transpose(2, 1, 0, 3)
        in_maps.append({"qn": np.ascontiguousarray(qn_[hs]), "qr": np.ascontiguousarray(qr_[hs]),
                        "kn": np.ascontiguousarray(kn_[hs]), "kr": np.ascontiguousarray(kr_),
                        "vv": v1.reshape(8, 128, 64 * 129), "consts": consts})
    nc = build_attn1(n_h)
    res = _launch(nc, in_maps)
    return res.results


def build_post1():
    nc = bass.Bass("TRN2", target_bir_lowering=False)
    dt = nc.dram_tensor
    oT1 = dt("oT1", [D, T], BF16, kind="ExternalInput").ap()
    zT1 = dt("zT1", [D, T], BF16, kind="ExternalInput").ap()
    x1T = dt("x1T", [D, T], F32, kind="ExternalInput").ap()
    vecs = dt("vecs", [128, 96 + 32], F32, kind="ExternalInput").ap()
    ones_in = dt("ones", [128, 128], F32, kind="ExternalInput").ap()
    w_out1 = dt("w_out1", [KC, 128, KC * 128], F32, kind="ExternalInput").ap()
    outT = dt("outT", [D, T], F32, kind="ExternalOutput").ap()
    x2T = dt("x2T", [D, T], F32).ap()
    kb = KB(nc)
    banks = [kb.psum(f"bank{i}", [128, 512]) for i in range(8)]
    vsb = kb.sbuf("vsb", [128, 128], F32)
    ones_bf = kb.sbuf("ones_bf", [128, 128], BF16)
    t_c = load_consts(kb, [(vsb[:], vecs)])
    dc = DmaBuf(kb, "ones")
    dc.dma("pool", ones_bf[:], ones_in)
    mod1, gf = vsb[:, 0:96], vsb[:, 96:128]
    kb.wait("dve", t_c)
    kb.wait("act", t_c)
    kb.wait("pe", dc.ticket())
    TP = T
    gbuf = kb.sbuf("gbuf", [128, KC, TP], BF16)
    wbufs = [kb.sbuf(f"wb{i}", [128, KC * 128], BF16) for i in range(2)]
    P = alloc_post(kb, "pb", TP, nld=1)
    sqb = [kb.sbuf(f"sqb{i}", [128, 512], BF16) for i in range(2)]
    sq_free = [None, None]
    sd = kb.sbuf("sd", [128, 512], F32)
    rstd = kb.sbuf("rstd", [128, TP], F32)
    TT = TP // 512
    V = nc.vector
    st = {"pend": [], "n": 0}

    def flush_pend():
        for fn in st["pend"]:
            fn()
        st["pend"] = []

    def extra(c, tt, xnew, tk):
        flush_pend()
        i = st["n"] % 2
        st["n"] += 1
        kb.wait("act", tk, sq_free[i])
        t2 = kb.sig("act", nc.scalar.activation(out=sqb[i][:, :], in_=xnew, func=AF.Square))

        def mm(i=i, t2=t2, c=c, tt=tt):
            kb.wait("pe", t2)
            t3 = kb.sig("pe", nc.tensor.matmul(banks[4 + tt][:, :], lhsT=ones_bf[:, :], rhs=sqb[i][:, :],
                                               start=(c == 0), stop=(c == KC - 1)))
            sq_free[i] = t3
            st["last"] = t3
        st["pend"].append(mm)

    stores = emit_post(kb, nc, P, oT=oT1, zT=zT1, xT=x1T, w_out=w_out1, gate=mod1[:, 64:96], x_out=x2T, gbuf=gbuf,
                       wbufs=wbufs, banks=banks, tag="pb", toff=0, TH=TP, evac_extra=extra, bank_ids=[0, 1, 2, 3])
    flush_pend()
    t_r = None
    for tt in range(TT):
        kb.wait("act", st["last"], t_r)
        t = kb.sig("act", nc.scalar.activation(out=sd[:, :], in_=banks[4 + tt][:, :], func=AF.Sqrt, scale=1.0 / D, bias=EPS))
        kb.wait("dve", t)
        t_r = kb.sig("dve", V.reciprocal(out=rstd[:, tt * 512:(tt + 1) * 512], in_=sd[:, :]))
    xl, xld, stage = P["xl"], P["xld"], P["stage"]
    kb.wait("sp", *stores)
    for c in range(KC):
        i = c % 2
        kb.wait("sp", P["xfree"][i])
        xld[i].dma("sp", xl[i][:, :], x2T[c * 128:(c + 1) * 128, :])
        buf, t_prev = stage.next()
        kb.wait("dve", xld[i].ticket(), t_prev, t_r)
        t_fin = kb.sig("dve", V.scalar_tensor_tensor(out=buf[:, :], in0=xl[i][:, :], scalar=gf[:, c:c + 1], in1=rstd[:, :],
                                                      op0=ALU.mult, op1=ALU.mult))
        P["xfree"][i] = t_fin
        stage.store(outT[c * 128:(c + 1) * 128, :], buf[:, :], t_fin)
    kb.wait("sp", *stage.all_tickets())
    kb.close()
    return nc


def run_post1(oT_list, zT_list, x1T_list, mod1, g_final, w_out1_l):
    ones = np.ones((128, 128), np.float32)
    in_maps = []
    for core in range(NCORES):
        b = core // 4
        vecs = np.concatenate([colT(mod1[b]), colT(g_final)], axis=1)
        in_maps.append({"oT1": oT_list[core], "zT1": zT_list[core], "x1T": x1T_list[core],
                        "vecs": np.ascontiguousarray(vecs), "ones": ones, "w_out1": w_out1_l})
    nc = build_post1()
    res = _launch(nc, in_maps)
    return res.results


def scatter_heads_to_tok(attn_res, b):
    full = np.concatenate([attn_res[b * 4 + g]["oT"] for g in range(4)], axis=0)
    return full


def kernel(x, c, positions, g_norm, w_ada, b_ada, rel_bias, w_in_even, b_forget, w_out_even, w_in_odd, g_q_lora,
           g_kv_lora, w_q_b, w_kv_b, w_out_odd, g_final):
    x = np.asarray(x, np.float32)
    mod = run_mod(np.asarray(c), np.asarray(w_ada), np.asarray(b_ada))
    pre = run_pre0(x, mod[0], np.asarray(g_norm)[0], np.asarray(w_in_even)[0])
    a0 = run_attn0(pre, np.asarray(rel_bias), np.asarray(b_forget)[0])
    xs = tok_shard_T(x)
    oT0, zT0 = [], []
    for core in range(NCORES):
        b, q = core // 4, core % 4
        tsl = slice(q * T, (q + 1) * T)
        oa = np.concatenate([a0[b * 4 + g]["oT"][0:512, tsl] for g in range(4)], axis=0)
        ob = np.concatenate([a0[b * 4 + g]["oT"][512:1024, tsl] for g in range(4)], axis=0)
        oT0.append(np.ascontiguousarray(np.concatenate([oa, ob], axis=0)))
        zT0.append(pre[core]["zT"])
    Wd = mid_weights(np.asarray(w_out_even)[0], np.asarray(w_in_odd)[0], np.asarray(w_q_b)[0], np.asarray(w_kv_b)[0])
    mid = run_mid(oT0, zT0, xs, mod, np.asarray(g_norm)[1], np.asarray(g_q_lora)[0], np.asarray(g_kv_lora)[0],
                  np.asarray(positions), Wd)
    a1 = run_attn1(mid)
    oT1 = []
    for core in range(NCORES):
        b, q = core // 4, core % 4
        tsl = slice(q * T, (q + 1) * T)
        oT1.append(np.ascontiguousarray(np.concatenate([a1[b * 4 + g]["oT"][:, tsl] for g in range(4)], axis=0)))
    fin = run_post1(oT1, [m["zT1"] for m in mid], [m["x1T"] for m in mid], mod[1], np.asarray(g_final),
                    lay_fm(np.asarray(w_out_odd)[0], 128))
    out = np.empty((NB, S, D), np.float32)
    for core in range(NCORES):
        b, q = core // 4, core % 4
        out[b, q * T:(q + 1) * T, :] = fin[core]["outT"].T
    if DEBUG.get("stash") is not None:
        DEBUG["stash"].update(dict(mod=mod, pre=pre, a0=a0, mid=mid, a1=a1, oT0=oT0, oT1=oT1))
    return out
```

```python
import contextlib
import math
import numpy as np
import ml_dtypes
import concourse.bass as bass
import concourse.mybir as mybir
from concourse.bass_utils import run_bass_kernel_spmd

F32 = mybir.dt.float32
BF16 = mybir.dt.bfloat16
I32 = mybir.dt.int32
AF = mybir.ActivationFunctionType
ALU = mybir.AluOpType
NPBF = ml_dtypes.bfloat16

D = 4096
S = 8192
NB = 2
KC = D // 128
T = 2048
NCORES = 8
EPS = 1e-6
NEG = -30000.0
DEBUG = {}


class KB:
    def __init__(self, nc):
        self.nc = nc
        self.es = contextlib.ExitStack()
        self.root_es = self.es
        self.eng = {"pe": nc.tensor, "act": nc.scalar, "dve": nc.vector, "pool": nc.gpsimd, "sp": nc.sync}
        self.cur = {}
        self.cnt = {}
        self.waited = {}
        self.nsem = 0
        self.bank_free = [None] * 8
        for e in self.eng:
            self._fresh(e)

    def _fresh(self, e):
        self.nsem += 1
        self.cur[e] = self.root_es.enter_context(self.nc.semaphore(f"t_{e}_{self.nsem}"))
        self.cnt[e] = 0

    def sem(self, name):
        self.nsem += 1
        return self.root_es.enter_context(self.nc.semaphore(f"{name}_{self.nsem}"))

    def sbuf(self, name, shape, dt):
        return self.es.enter_context(self.nc.sbuf_tensor(name, list(shape), dt))

    def psum(self, name, shape, dt=F32):
        return self.es.enter_context(self.nc.psum_tensor(name, list(shape), dt))

    def sig(self, e, instr):
        if self.cnt[e] >= 30000:
            self._fresh(e)
        self.cnt[e] += 1
        instr.then_inc(self.cur[e], 1)
        self.last = getattr(self, "last", {})
        self.last[e] = (self.cur[e], self.cnt[e], e)
        return self.last[e]

    def barrier(self, *extra):
        tks = [t for t in getattr(self, "last", {}).values()] + list(extra)
        for e in self.eng:
            self.wait(e, *tks)

    def wait(self, e, *tickets):
        for t in tickets:
            if t is None:
                continue
            sem, v = t[0], t[1]
            key = (e, id(sem))
            if self.waited.get(key, 0) >= v:
                continue
            self.waited[key] = v
            self.eng[e].wait_ge(sem, v)

    def close(self):
        self.root_es.close()


class DmaBuf:
    def __init__(self, kb, name):
        self.kb = kb
        self.sem = kb.sem("d_" + name)
        self.count = 0

    def dma(self, e, out, in_):
        self.kb.eng[e].dma_start(out=out, in_=in_).then_inc(self.sem, 16)
        self.count += 16

    def ticket(self):
        return (self.sem, self.count, "dma")


def _launch(nc, in_maps):
    if DEBUG.get("trace"):
        res = run_bass_kernel_spmd(nc, in_maps, core_ids=list(range(NCORES)), trace=True)
        DEBUG.setdefault("times", []).append(res.exec_time_ns)
        print("LAUNCH exec_time_ns", res.exec_time_ns, flush=True)
        return res
    return run_bass_kernel_spmd(nc, in_maps, core_ids=list(range(NCORES)))


def _bf16_round(a):
    return a.astype(NPBF)


def build_mod():
    nc = bass.Bass("TRN2", target_bir_lowering=False)
    NCOL = 3072
    cT = nc.dram_tensor("cT", [128, KC * NB], F32, kind="ExternalInput").ap()
    wada = nc.dram_tensor("wada", [6, 128, KC * 512], F32, kind="ExternalInput").ap()
    bada = nc.dram_tensor("bada", [NB, NCOL], F32, kind="ExternalInput").ap()
    out = nc.dram_tensor("modp", [NB, NCOL], F32, kind="ExternalOutput").ap()
    kb = KB(nc)
    c_in = kb.sbuf("c_in", [128, KC * NB], F32)
    sc = kb.sbuf("sc", [128, KC * NB], F32)
    bsb = kb.sbuf("bsb", [NB, NCOL], F32)
    osb = kb.sbuf("osb", [NB, NCOL], F32)
    wb = [kb.sbuf(f"wb{i}", [128, KC * 512], F32) for i in range(2)]
    ps = [kb.psum(f"ps{i}", [128, 512]) for i in range(2)]
    ld0 = DmaBuf(kb, "c")
    ld0.dma("sp", c_in[:], cT)
    ld0.dma("sp", bsb[:], bada)
    wl = [DmaBuf(kb, f"w{i}") for i in range(2)]
    kb.wait("act", ld0.ticket())
    t_sc = kb.sig("act", nc.scalar.activation(out=sc[:], in_=c_in[:], func=AF.Silu))
    pe_done = [None, None]
    ev_done = [None, None]
    for j in range(6):
        b = j % 2
        kb.wait("sp", pe_done[b])
        wl[b].dma("sp", wb[b][:], wada[j])
        if j == 0:
            kb.wait("pe", t_sc)
        kb.wait("pe", wl[b].ticket(), ev_done[b])
        for kc in range(KC):
            mm = nc.tensor.matmul(ps[b][0:NB, :], lhsT=sc[:, kc * NB:(kc + 1) * NB],
                                  rhs=wb[b][:, kc * 512:(kc + 1) * 512], start=(kc == 0), stop=(kc == KC - 1))
        pe_done[b] = kb.sig("pe", mm)
        kb.wait("dve", pe_done[b], ld0.ticket())
        ev_done[b] = kb.sig("dve", nc.vector.tensor_tensor(out=osb[:, j * 512:(j + 1) * 512], in0=ps[b][0:NB, :],
                                                            in1=bsb[:, j * 512:(j + 1) * 512], op=ALU.add))
    st = DmaBuf(kb, "st")
    kb.wait("sp", ev_done[0], ev_done[1])
    st.dma("sp", out, osb[:])
    kb.wait("sp", st.ticket())
    kb.close()
    return nc


def run_mod(c, w_ada, b_ada):
    cT = np.ascontiguousarray(c.reshape(NB, KC, 128).transpose(2, 1, 0).reshape(128, KC * NB))
    in_maps = []
    for core in range(NCORES):
        l, q = core // 4, core % 4
        w = w_ada[l][:, q * 3072:(q + 1) * 3072]
        w = w.reshape(KC, 128, 6, 512).transpose(2, 1, 0, 3).reshape(6, 128, KC * 512)
        b = np.broadcast_to(b_ada[l][None, q * 3072:(q + 1) * 3072], (NB, 3072))
        in_maps.append({"cT": cT, "wada": np.ascontiguousarray(w), "bada": np.ascontiguousarray(b)})
    nc = build_mod()
    res = _launch(nc, in_maps)
    mod = np.zeros((2, NB, 3 * D), np.float32)
    for core in range(NCORES):
        l, q = core // 4, core % 4
        mod[l, :, q * 3072:(q + 1) * 3072] = res.results[core]["modp"]
    return mod


def emit_norm(kb, xT_dram, hT, ones_bf, A, B, pbank, tag, sq_scale=1.0 / D):
    nc = kb.nc
    nkc = hT.shape[1]
    TN = hT.shape[2]
    TT = TN // 512
    xv = xT_dram.rearrange("(k p) t -> k p t", p=128)
    xb = [kb.sbuf(f"{tag}_x{i}", [128, 512], F32) for i in range(3)]
    xl = [DmaBuf(kb, f"{tag}_x{i}") for i in range(3)]
    x_free = [None] * 3
    sq = [kb.sbuf(f"{tag}_sq{i}", [128, 512], BF16) for i in range(2)]
    sq_free = [None] * 2
    tmp = [kb.sbuf(f"{tag}_tmp{i}", [128, 512], F32) for i in range(2)]
    tmp_free = [None] * 2
    rstd = kb.sbuf(f"{tag}_rstd", [128, TN], F32)
    sd = kb.sbuf(f"{tag}_sd", [128, 512], F32)
    n = 0
    out_t = []
    for tt in range(TT):
        tsl = slice(tt * 512, (tt + 1) * 512)
        for kc in range(nkc):
            i = n % 3
            n += 1
            kb.wait("sp", x_free[i])
            xl[i].dma("sp", xb[i][:], xv[kc, :, tsl])
            j = kc % 2
            kb.wait("act", xl[i].ticket(), sq_free[j])
            t_sq = kb.sig("act", nc.scalar.activation(out=sq[j][:], in_=xb[i][:], func=AF.Square))
            x_free[i] = t_sq
            kb.wait("pe", t_sq)
            if kc == 0 and tt > 0:
                kb.wait("pe", t_sd)
            mm = nc.tensor.matmul(pbank[:, :], lhsT=ones_bf[:, :], rhs=sq[j][:], start=(kc == 0), stop=(kc == nkc - 1))
            sq_free[j] = kb.sig("pe", mm)
        t_ss = sq_free[(nkc - 1) % 2]
        kb.wait("act", t_ss)
        t_sd = kb.sig("act", nc.scalar.activation(out=sd[:], in_=pbank[:, :], func=AF.Sqrt, scale=sq_scale, bias=EPS))
        kb.wait("dve", t_sd)
        t_r = kb.sig("dve", nc.vector.reciprocal(out=rstd[:, tsl], in_=sd[:]))
        for kc in range(nkc):
            i = n % 3
            n += 1
            kb.wait("sp", x_free[i])
            xl[i].dma("sp", xb[i][:], xv[kc, :, tsl])
            j = kc % 2
            kb.wait("dve", xl[i].ticket(), tmp_free[j], t_r)
            t_m = kb.sig("dve", nc.vector.scalar_tensor_tensor(out=tmp[j][:], in0=xb[i][:], scalar=A[:, kc:kc + 1],
                                                                in1=rstd[:, tsl], op0=ALU.mult, op1=ALU.mult))
            x_free[i] = t_m
            kb.wait("act", t_m)
            t_h = kb.sig("act", nc.scalar.activation(out=hT[:, kc, tsl], in_=tmp[j][:], func=AF.Identity,
                                                     bias=B[:, kc:kc + 1]))
            tmp_free[j] = t_h
        out_t.append(t_h)
    kb.norm_bank_ticket = t_sd
    return out_t


def emit_linear_fm(kb, w_dram, nchunks, hT, nkc, banks, evac, tag, h_ready=None, mrows=128, wbufs=None, bank_ids=None,
                   group=1):
    nc = kb.nc
    TT = hT.shape[2] // 512
    if wbufs is None:
        wbufs = [kb.sbuf(f"{tag}_w{i}", [128, nkc * mrows], BF16) for i in range(2)]
    wl = [DmaBuf(kb, f"{tag}_w{i}") for i in range(2)]
    w_free = kb.__dict__.setdefault("wfree", {}).setdefault(id(wbufs[0]), [None, None])
    bank_free = kb.bank_free
    if bank_ids is None:
        bank_ids = list(range(8))
    X = nkc * mrows
    ngroups = (nchunks + group - 1) // group

    def load(gi):
        b = gi % 2
        n = min(group, nchunks - gi * group)
        kb.wait("pool", w_free[b])
        if n == 1:
            wl[b].dma("pool", wbufs[b][:, 0:X], w_dram[gi * group])
        else:
            wl[b].dma("pool", wbufs[b][:, 0:n * X].rearrange("p (g n) -> p g n", g=n),
                      w_dram[gi * group:gi * group + n].rearrange("g p n -> p g n"))

    load(0)
    for c in range(nchunks):
        gi, ci = c // group, c % group
        b = gi % 2
        if ci == 0:
            if gi + 1 < ngroups:
                load(gi + 1)
            kb.wait("pe", wl[b].ticket())
        for tt in range(TT):
            bk = bank_ids[(c * TT + tt) % len(bank_ids)]
            kb.wait("pe", bank_free[bk])
            if h_ready is not None:
                kb.wait("pe", h_ready[tt])
            for kc in range(nkc):
                o = ci * X + kc * mrows
                mm = nc.tensor.matmul(banks[bk][0:mrows, :], lhsT=wbufs[b][:, o:o + mrows],
                                      rhs=hT[:, kc, tt * 512:(tt + 1) * 512], start=(kc == 0), stop=(kc == nkc - 1))
            t_mm = kb.sig("pe", mm)
            bank_free[bk] = evac(c, tt, banks[bk], t_mm)
        w_free[b] = t_mm
    kb.bank_free_last_mm = t_mm
    kb.w_free_tickets = list(w_free)
    return bank_free


class OutStage:
    def __init__(self, kb, name, shape, dt, n=2):
        self.kb = kb
        self.bufs = [kb.sbuf(f"{name}{i}", shape, dt) for i in range(n)]
        self.dmas = [DmaBuf(kb, f"{name}{i}") for i in range(n)]
        self.n = n
        self.k = -1

    def next(self):
        self.k += 1
        i = self.k % self.n
        return self.bufs[i], self.dmas[i].ticket()

    def store(self, dst, src, *tickets):
        i = self.k % self.n
        self.kb.wait("sp", *tickets)
        self.dmas[i].dma("sp", dst, src)

    def all_tickets(self):
        return [d.ticket() for d in self.dmas]


def emit_linear_tm(kb, w_dram, ntiles, hT, nkc, NW, banks, evac, tag, wbufs, bank_ids=None):
    nc = kb.nc
    NT = hT.shape[2] // 128
    wl = [DmaBuf(kb, f"{tag}_w{i}") for i in range(2)]
    w_free = kb.__dict__.setdefault("wfree", {}).setdefault(id(wbufs[0]), [None, None])
    bank_free = kb.bank_free
    if bank_ids is None:
        bank_ids = list(range(8))
    n = 0

    def load(j):
        b = j % 2
        kb.wait("pool", w_free[b])
        wl[b].dma("pool", wbufs[b][:, 0:nkc * NW], w_dram[j])

    load(0)
    for j in range(ntiles):
        b = j % 2
        if j + 1 < ntiles:
            load(j + 1)
        kb.wait("pe", wl[b].ticket())
        for t in range(NT):
            bk = bank_ids[n % len(bank_ids)]
            n += 1
            kb.wait("pe", bank_free[bk])
            for kc in range(nkc):
                mm = nc.tensor.matmul(banks[bk][:, 0:NW], lhsT=hT[:, kc, t * 128:(t + 1) * 128],
                                      rhs=wbufs[b][:, kc * NW:(kc + 1) * NW], start=(kc == 0), stop=(kc == nkc - 1))
            t_mm = kb.sig("pe", mm)
            bank_free[bk] = evac(j, t, banks[bk], t_mm)
        w_free[b] = t_mm
    kb.bank_free_last_mm = t_mm
    kb.w_free_tickets = list(w_free)
    return bank_free


def load_consts(kb, aps, eng="sp"):
    d = DmaBuf(kb, "consts")
    for o, i in aps:
        d.dma(eng, o, i)
    return d.ticket()


NQK0 = 64
NZ = 32


def build_pre0():
    nc = bass.Bass("TRN2", target_bir_lowering=False)
    xT = nc.dram_tensor("xT", [D, T], F32, kind="ExternalInput").ap()
    modT = nc.dram_tensor("modT", [128, 96], F32, kind="ExternalInput").ap()
    gT = nc.dram_tensor("gT", [128, KC], F32, kind="ExternalInput").ap()
    ones_in = nc.dram_tensor("ones", [128, 128], F32, kind="ExternalInput").ap()
    w_fm = nc.dram_tensor("w_fm", [DEBUG.get("pre0_nfm", NQK0 + NZ), 128, KC * 128], F32, kind="ExternalInput").ap()
    w_f = nc.dram_tensor("w_f", [1, 128, KC * 16], F32, kind="ExternalInput").ap()
    w_tm = nc.dram_tensor("w_tm", [16, 128, KC * 256], F32, kind="ExternalInput").ap()
    qkT = nc.dram_tensor("qkT", [NQK0 * 128, T], BF16, kind="ExternalOutput").ap()
    zT = nc.dram_tensor("zT", [D, T], BF16, kind="ExternalOutput").ap()
    v = nc.dram_tensor("v", [T, D], BF16, kind="ExternalOutput").ap()
    fT = nc.dram_tensor("fT", [16, T], F32, kind="ExternalOutput").ap()
    kb = KB(nc)
    hT = kb.sbuf("hT", [128, KC, T], BF16)
    ones_bf = kb.sbuf("ones_bf", [128, 128], BF16)
    mod_sb = kb.sbuf("mod_sb", [128, 96], F32)
    g_sb = kb.sbuf("g_sb", [128, KC], F32)
    A = kb.sbuf("A", [128, KC], F32)
    banks = [kb.psum(f"bank{i}", [128, 512]) for i in range(8)]
    t_c = load_consts(kb, [(mod_sb[:], modT), (g_sb[:], gT)])
    dc = DmaBuf(kb, "ones")
    dc.dma("pool", ones_bf[:], ones_in)
    kb.wait("dve", t_c)
    t_A = kb.sig("dve", nc.vector.scalar_tensor_tensor(out=A[:], in0=mod_sb[:, 32:64], scalar=1.0, in1=g_sb[:],
                                                        op0=ALU.add, op1=ALU.mult))
    kb.wait("act", t_c)
    kb.wait("pe", dc.ticket())
    wbufs = [kb.sbuf(f"fm_w{i}", [128, KC * 128], BF16) for i in range(2)]
    stage = OutStage(kb, "fmst", [128, T], BF16)
    fstage = kb.sbuf("fstage", [16, T], F32)
    with contextlib.ExitStack() as sub:
        kb_es, kb.es = kb.es, sub
        h_ready = emit_norm(kb, xT, hT, ones_bf, A, mod_sb[:, 0:32], banks[0], "n0")
        kb.es = kb_es
    kb.bank_free[0] = kb.norm_bank_ticket
    tmw = [kb.sbuf(f"tm_w{i}", [128, KC * 256], BF16) for i in range(2)]
    vstage = OutStage(kb, "vst", [128, 16, 256], BF16, n=1)

    state = {}
    qscale = 1.0 / math.sqrt(128.0)

    def evac_fm(c, tt, bank, t_mm):
        if tt == 0:
            buf, t_prev = stage.next()
            state["buf"] = buf
            state["prev"] = t_prev
        buf = state["buf"]
        sl = slice(tt * 512, (tt + 1) * 512)
        if c >= NQK0:
            e = "act"
            kb.wait(e, t_mm, state["prev"])
            tk = kb.sig(e, nc.scalar.activation(out=buf[:, sl], in_=bank[:, :], func=AF.Silu))
        else:
            sc = qscale if (c < 16 or 32 <= c < 48) else 1.0
            e = "dve" if (tt % 2 == 0) else "act"
            kb.wait(e, t_mm, state["prev"])
            if e == "dve":
                tk = kb.sig(e, nc.vector.tensor_scalar(out=buf[:, sl], in0=bank[:, :], scalar1=sc, scalar2=None,
                                                       op0=ALU.mult))
            else:
                tk = kb.sig(e, nc.scalar.activation(out=buf[:, sl], in_=bank[:, :], func=AF.Copy, scale=sc))
        state.setdefault("tks", []).append(tk)
        if tt == 3:
            dst = qkT[c * 128:(c + 1) * 128, :] if c < NQK0 else zT[(c - NQK0) * 128:(c - NQK0 + 1) * 128, :]
            stage.store(dst, buf[:, :], *state["tks"])
            state["tks"] = []
        return tk

    if DEBUG.get("pre0", 9) >= 2:
        emit_linear_fm(kb, w_fm, DEBUG.get("pre0_nfm", NQK0 + NZ), hT, KC, banks, evac_fm, "fm", h_ready=h_ready, wbufs=wbufs)

    fw = [kb.sbuf(f"f_w{i}", [128, KC * 16], BF16) for i in range(2)]
    ftk = []

    def evac_f(c, tt, bank, t_mm):
        kb.wait("dve", t_mm)
        tk = kb.sig("dve", nc.vector.tensor_copy(out=fstage[:, tt * 512:(tt + 1) * 512], in_=bank[0:16, :]))
        ftk.append(tk)
        return tk

    fst = DmaBuf(kb, "fst")
    if DEBUG.get("pre0", 9) >= 3:
        emit_linear_fm(kb, w_f, 1, hT, KC, banks, evac_f, "ff", mrows=16, wbufs=fw)
        kb.wait("sp", *ftk)
        fst.dma("sp", fT, fstage[:, :])

    vs = {}

    def evac_tm(j, t, bank, t_mm):
        if t == 0:
            buf, t_prev = vstage.next()
            vs["buf"], vs["prev"], vs["tks"] = buf, t_prev, []
        e = "dve" if (t % 2 == 0) else "act"
        kb.wait(e, t_mm, vs["prev"])
        if e == "dve":
            tk = kb.sig(e, nc.vector.tensor_copy(out=vs["buf"][:, t, :], in_=bank[:, 0:256]))
        else:
            tk = kb.sig(e, nc.scalar.copy(out=vs["buf"][:, t, :], in_=bank[:, 0:256]))
        vs["tks"].append(tk)
        if t == 15:
            dst = v[:, j * 256:(j + 1) * 256].rearrange("(t p) c -> p t c", p=128)
            vstage.store(dst, vs["buf"][:, :, :], *vs["tks"])
        return tk

    if DEBUG.get("pre0", 9) >= 4:
        emit_linear_tm(kb, w_tm, 16, hT, KC, 256, banks, evac_tm, "tm", tmw)
    kb.wait("sp", fst.ticket(), *stage.all_tickets(), *vstage.all_tickets())
    kb.close()
    return nc


def lay_fm(w, ncols_chunk=128):
    K_, N_ = w.shape
    kc = K_ // 128
    c = N_ // ncols_chunk
    return np.ascontiguousarray(w.reshape(kc, 128, c, ncols_chunk).transpose(2, 1, 0, 3).reshape(c, 128, kc * ncols_chunk))


def tok_shard_T(x):
    out = []
    for core in range(NCORES):
        b, q = core // 4, core % 4
        out.append(np.ascontiguousarray(x[b, q * T:(q + 1) * T, :].T))
    return out


def colT(vec):
    return np.ascontiguousarray(vec.reshape(-1, 128).T)


def run_pre0(x, mod0, g0, w_in):
    w_qk = w_in[:, 0:12288].reshape(D, 6, 2048)
    w_fm = np.concatenate([w_qk[:, 0], w_qk[:, 1], w_qk[:, 3], w_qk[:, 4], w_in[:, 12288:16384]], axis=1)
    w_tm = np.concatenate([w_qk[:, 2], w_qk[:, 5]], axis=1)
    w_fm_l = lay_fm(w_fm, 128)
    w_tm_l = lay_fm(w_tm, 256)
    w_f_l = lay_fm(w_in[:, 16384:16400], 16)
    ones = np.ones((128, 128), np.float32)
    gT = colT(g0)
    xs = tok_shard_T(x)
    in_maps = []
    for core in range(NCORES):
        b = core // 4
        in_maps.append({"xT": xs[core], "modT": colT(mod0[b]), "gT": gT, "ones": ones,
                        "w_fm": w_fm_l[:DEBUG.get("pre0_nfm", NQK0 + NZ)], "w_f": w_f_l, "w_tm": w_tm_l})
    nc = build_pre0()
    res = _launch(nc, in_maps)
    return res.results


class Stream:
    def __init__(self, kb, sbanks, pbufs, LA=2):
        self.kb, self.sbanks, self.pbufs, self.LA = kb, sbanks, pbufs, LA
        self.s_free = [None] * len(sbanks)
        self.p_free = [None] * len(pbufs)
        self.n = 0
        self.pending = []
        self.deferred = []

    def _run_deferred(self, force=False):
        keep = []
        for due, fn in self.deferred:
            if force or due <= self.n:
                fn()
            else:
                keep.append((due, fn))
        self.deferred = keep

    def defer(self, delay, fn):
        self.deferred.append((self.n + delay, fn))

    def _pv(self):
        kb = self.kb
        i, tile, t_act = self.pending.pop(0)
        pb = i % len(self.pbufs)
        kb.wait("pe", t_act, *tile.get("pv_waits", lambda: [])())
        mm = tile["pv"](self.pbufs[pb])
        t_pv = kb.sig("pe", mm)
        self.p_free[pb] = t_pv
        self.last_pv = t_pv
        if "post" in tile:
            tile["post"](t_pv)

    def push(self, tile):
        kb = self.kb
        i = self.n
        b = i % len(self.sbanks)
        pb = i % len(self.pbufs)
        self._run_deferred()
        kb.wait("pe", self.s_free[b], *tile.get("qk_waits", []))
        mm = tile["qk"](self.sbanks[b])
        t_qk = kb.sig("pe", mm)
        kb.wait("act", t_qk, self.p_free[pb], *tile.get("act_waits", []))
        a = tile["act"](self.sbanks[b], self.pbufs[pb])
        t_act = kb.sig("act", a)
        self.s_free[b] = t_act
        self.pending.append((i, tile, t_act))
        self.n += 1
        if len(self.pending) > self.LA:
            self._pv()

    def flush(self):
        while self.pending:
            self._pv()
        self._run_deferred(force=True)


def emit_causal_head(kb, st, nc, *, kT, qT, v1, kr=None, qr=None, fq3=None, bias_col=None, ones_bf=None,
                     ident_bf, cmask_bf, accs, acc_free, tr_bank, tr_free, osb, ostage, o_dst, op_waits,
                     act_waits, rinv):
    NQ = S // 512
    first = [True]
    for Q in range(NQ):
        aset = Q % 2
        q0 = Q * 512
        state = {"tks": []}
        for j in range(4 * Q + 4):
            r = j - 4 * Q
            c0 = 128 * r if r > 0 else 0
            diag = r >= 0

            def qk(bank, j=j, c0=c0, diag=diag, q0=q0):
                nc.tensor.matmul(bank[:, c0:512], lhsT=kT[:, j * 128:(j + 1) * 128], rhs=qT[:, q0 + c0:q0 + 512],
                                 start=True, stop=False)
                if diag:
                    nc.tensor.matmul(bank[:, c0:c0 + 128], lhsT=ident_bf[:, :], rhs=cmask_bf[:, :], start=False, stop=False)
                if kr is not None:
                    mm = nc.tensor.matmul(bank[:, c0:512], lhsT=kr[0:64, j * 128:(j + 1) * 128],
                                          rhs=qr[0:64, q0 + c0:q0 + 512], start=False, stop=True)
                else:
                    mm = nc.tensor.matmul(bank[:, c0:512], lhsT=ones_bf[:, :], rhs=fq3[:, q0 + c0:q0 + 512],
                                          start=False, stop=True)
                return mm

            def act(bank, pbuf, j=j, c0=c0):
                if bias_col is not None:
                    return nc.scalar.activation(out=pbuf[:, c0:512], in_=bank[:, c0:512], func=AF.Exp, bias=bias_col(j))
                return nc.scalar.activation(out=pbuf[:, c0:512], in_=bank[:, c0:512], func=AF.Exp)

            def pv(pbuf, j=j, c0=c0, Q=Q, aset=aset):
                mm = None
                for qs in range(c0 // 128, 4):
                    a = accs[aset][qs // 2]
                    off = (qs % 2) * 256
                    mm = nc.tensor.matmul(a[:, off:off + 129], lhsT=pbuf[:, qs * 128:(qs + 1) * 128], rhs=v1[:, j, 0:129],
                                          start=(j == 0 and qs % 2 == 0), stop=(j == 4 * Q + qs))
                return mm

            tile = {"qk": qk, "act": act, "pv": pv}
            if first[0]:
                tile["qk_waits"] = list(op_waits)
                tile["act_waits"] = list(act_waits)
                first[0] = False
            if j == 0:
                tile["pv_waits"] = (lambda aset=aset: [acc_free[aset]])
            if r == 3:
                def post(t_pv, Q=Q, aset=aset, q0=q0):
                    kb.wait("dve", t_pv, tr_free[0])
                    tks = []
                    for qs in range(4):
                        a = accs[aset][qs // 2]
                        off = (qs % 2) * 256
                        rc = rinv[:, aset * 4 + qs:aset * 4 + qs + 1]
                        t1 = kb.sig("dve", nc.vector.reciprocal(out=rc, in_=a[:, off + 128:off + 129]))
                        kb.wait("dve", t1)
                        t2 = kb.sig("dve", nc.vector.tensor_scalar(out=osb[:, qs * 128:(qs + 1) * 128], in0=a[:, off:off + 128],
                                                                   scalar1=rc, scalar2=None, op0=ALU.mult))
                        tks.append(t2)
                    acc_free[aset] = t2

                    def fin(tks=tks, q0=q0):
                        kb.wait("pe", *tks, tr_free[1])
                        for s4 in range(4):
                            tp = nc.tensor.transpose(out=tr_bank[:, s4 * 128:(s4 + 1) * 128],
                                                     in_=osb[:, s4 * 128:(s4 + 1) * 128], identity=ident_bf[:, :])
                        t3 = kb.sig("pe", tp)
                        tr_free[0] = t3
                        buf, t_prev = ostage.next()
                        kb.wait("act", t3, t_prev)
                        t4 = kb.sig("act", nc.scalar.copy(out=buf[:, :], in_=tr_bank[:, :]))
                        tr_free[1] = t4
                        ostage.store(o_dst[:, q0:q0 + 512], buf[:, :], t4)
                    st.defer(2, fin)
                tile["post"] = post
            st.push(tile)


DILS = (1, 4, 16)


def emit_dilated_head(kb, st, nc, *, kT, qT, vperm, bias_hi, bias_lo, hl, ones_bf, ident_bf, dbanks, slot_free,
                      accN, accZ, ostage, o_dst, op_waits, rz):
    first = [True]
    cnt = [0]
    for sb in range(S // 2048):
        for p, d in enumerate(DILS):
            nbpr = 64 // d
            for r in range(d):
                for n in range(sb * 16 // d, (sb + 1) * 16 // d):
                    base = r + d * 128 * n
                    QS = slice(base, base + 127 * d + 1, d)
                    PS = slice(base - 128 * d, base - d + 1, d)
                    ncol = 256 if n > 0 else 128
                    blk = r * nbpr + n
                    ph = p * 4 + hl
                    LS = slice(base - sb * 2048, base - sb * 2048 + 127 * d + 1, d)

                    def qk(bank, QS=QS, PS=PS, n=n, ncol=ncol, ph=ph):
                        nc.tensor.matmul(bank[:, 0:128], lhsT=kT[:, QS], rhs=qT[:, QS], start=True, stop=False)
                        if n > 0:
                            nc.tensor.matmul(bank[:, 128:256], lhsT=kT[:, PS], rhs=qT[:, QS], start=False, stop=False)
                        nc.tensor.matmul(bank[:, 0:ncol], lhsT=ident_bf[:, :], rhs=bias_hi[:, ph, 0:ncol], start=False, stop=False)
                        return nc.tensor.matmul(bank[:, 0:ncol], lhsT=ident_bf[:, :], rhs=bias_lo[:, ph, 0:ncol],
                                                start=False, stop=True)

                    def act(bank, pbuf, ncol=ncol):
                        return nc.scalar.activation(out=pbuf[:, 0:ncol], in_=bank[:, 0:ncol], func=AF.Exp)

                    slot = cnt[0] % 4
                    cnt[0] += 1
                    dbank = dbanks[slot]
                    off = 0

                    def pv(pbuf, p=p, blk=blk, n=n, dbank=dbank, off=off):
                        nc.tensor.matmul(dbank[:, off:off + 128], lhsT=vperm[p][:, blk, :], rhs=pbuf[:, 0:128],
                                         start=True, stop=(n == 0))
                        if n > 0:
                            nc.tensor.matmul(dbank[:, off:off + 128], lhsT=vperm[p][:, blk - 1, :], rhs=pbuf[:, 128:256],
                                             start=False, stop=True)
                        mm = nc.tensor.matmul(dbank[:, off + 128:off + 256], lhsT=ones_bf[:, :], rhs=pbuf[:, 0:128],
                                              start=False, stop=(n == 0))
                        if n > 0:
                            mm = nc.tensor.matmul(dbank[:, off + 128:off + 256], lhsT=ones_bf[:, :], rhs=pbuf[:, 128:256],
                                                  start=False, stop=True)
                        return mm

                    def post(t_pv, p=p, LS=LS, dbank=dbank, off=off, slot=slot):
                        kb.wait("dve", t_pv)
                        if p == 0:
                            nc.vector.tensor_copy(out=accN[:, LS], in_=dbank[:, off:off + 128])
                            tk = kb.sig("dve", nc.vector.tensor_copy(out=accZ[:, LS], in_=dbank[:, off + 128:off + 256]))
                        else:
                            nc.vector.tensor_tensor(out=accN[:, LS], in0=accN[:, LS], in1=dbank[:, off:off + 128], op=ALU.add)
                            tk = kb.sig("dve", nc.vector.tensor_tensor(out=accZ[:, LS], in0=accZ[:, LS],
                                                                       in1=dbank[:, off + 128:off + 256], op=ALU.add))
                        slot_free[slot] = tk

                    tile = {"qk": qk, "act": act, "pv": pv, "post": post,
                            "pv_waits": (lambda slot=slot: [slot_free[slot]])}
                    if first[0]:
                        tile["qk_waits"] = list(op_waits)
                        first[0] = False
                    st.push(tile)
        st.flush()
        last = [slot_free[i] for i in range(4)]
        kb.wait("dve", *last)
        t1 = kb.sig("dve", nc.vector.reciprocal(out=rz[:, :], in_=accZ[:, :]))
        buf, t_prev = ostage.next()
        kb.wait("dve", t1, t_prev)
        t2 = kb.sig("dve", nc.vector.tensor_tensor(out=buf[:, :], in0=accN[:, :], in1=rz[:, :], op=ALU.mult))
        ostage.store(o_dst[:, sb * 2048:(sb + 1) * 2048], buf[:, :], t2)
        kb.dil_last = t2


def attn_common(kb, nc, consts_in):
    c = {}
    c["ident_bf"] = kb.sbuf("ident_bf", [128, 128], BF16)
    c["cmask_bf"] = kb.sbuf("cmask_bf", [128, 128], BF16)
    c["ones_bf"] = kb.sbuf("ones_bf", [128, 128], BF16)
    c["U_f"] = kb.sbuf("U_f", [128, 128], F32)
    c["ones_f"] = kb.sbuf("ones_f", [128, 128], F32)
    d = DmaBuf(kb, "attc")
    d.dma("pool", c["ident_bf"][:], consts_in[0])
    d.dma("pool", c["cmask_bf"][:], consts_in[1])
    d.dma("pool", c["ones_bf"][:], consts_in[3])
    d.dma("sp", c["U_f"][:], consts_in[2])
    d.dma("sp", c["ones_f"][:], consts_in[3])
    c["ticket"] = d.ticket()
    sb = [kb.psum(f"sbank{i}", [128, 512]) for i in range(3)]
    c["accs"] = [[kb.psum(f"acc{a}{i}", [128, 512]) for i in range(2)] for a in range(2)]
    c["tr_bank"] = kb.psum("tr_bank", [128, 512], BF16)
    pb = [kb.sbuf(f"pbuf{i}", [128, 512], BF16) for i in range(4)]
    c["st"] = Stream(kb, sb, pb, LA=2)
    c["sbanks"] = sb
    c["osb"] = kb.sbuf("osb", [128, 512], BF16)
    c["rinv"] = kb.sbuf("rinv", [128, 8], F32)
    c["ostage"] = OutStage(kb, "ost", [128, 512], BF16, n=3)
    c["acc_free"] = [None, None]
    c["tr_free"] = [None, None]
    return c


def build_attn0(n_a=4, n_b=4):
    nc = bass.Bass("TRN2", target_bir_lowering=False)
    qkA = nc.dram_tensor("qkA", [2, 512, S], BF16, kind="ExternalInput").ap()
    vaP = nc.dram_tensor("vaP", [3, 4, 128, 64 * 128], BF16, kind="ExternalInput").ap()
    qkB = nc.dram_tensor("qkB", [2, 512, S], BF16, kind="ExternalInput").ap()
    vb = nc.dram_tensor("vb", [4, 128, 64 * 129], BF16, kind="ExternalInput").ap()
    fl = nc.dram_tensor("fl", [128, 256], F32, kind="ExternalInput").ap()
    bfr = nc.dram_tensor("bfr", [128, 256], F32, kind="ExternalInput").ap()
    biasm = nc.dram_tensor("biasm", [128, 12 * 256], F32, kind="ExternalInput").ap()
    consts_in = nc.dram_tensor("consts", [4, 128, 128], F32, kind="ExternalInput").ap()
    oT = nc.dram_tensor("oT", [1024, S], BF16, kind="ExternalOutput").ap()
    fscr = nc.dram_tensor("fscr", [3, 4, S], BF16).ap()
    kb = KB(nc)
    C = attn_common(kb, nc, consts_in)
    st = C["st"]
    qbuf = [kb.sbuf(f"qbuf{i}", [128, S], BF16) for i in range(2)]
    kbuf = [kb.sbuf(f"kbuf{i}", [128, S], BF16) for i in range(2)]
    qkl = [DmaBuf(kb, f"qkl{i}") for i in range(2)]
    vperm = [kb.sbuf(f"vperm{p}", [128, 64, 128], BF16) for p in range(3)]
    vpl = DmaBuf(kb, "vpl")
    v1 = [kb.sbuf("v1_0", [128, 64, 129], BF16)]
    fq3 = [kb.sbuf("fq3_0", [128, S], BF16)]
    fql = DmaBuf(kb, "fql")
    bl = [DmaBuf(kb, f"bl{i}") for i in range(2)]
    accN = kb.sbuf("accN", [128, 2048], F32)
    accZ = kb.sbuf("accZ", [128, 2048], F32)
    dstage = OutStage(kb, "dst", [128, 2048], BF16, n=1)
    bias32 = kb.sbuf("bias32", [128, 4 * 256], F32)
    rz = accZ
    bias_hi = kb.sbuf("bias_hi", [128, 12, 256], BF16)
    bias_lo = kb.sbuf("bias_lo", [128, 12, 256], BF16)
    fl_sb = kb.sbuf("fl_sb", [128, 256], F32)
    bf_sb = kb.sbuf("bf_sb", [128, 256], F32)
    lsb = kb.sbuf("lsb", [128, 256], F32)
    tot = kb.sbuf("tot", [128, 256], F32)
    offs = kb.sbuf("offs", [128, 256], F32)
    fneg = kb.sbuf("fneg", [128, 256], F32)
    r1 = kb.sbuf("r1", [128, 256], F32)
    parts = kb.sbuf("parts", [128, 3, 256], BF16)
    frT = kb.sbuf("frT", [64, 3, 512], BF16)

    sm = DmaBuf(kb, "small")
    sm.dma("sp", fl_sb[:], fl)
    sm.dma("sp", bf_sb[:], bfr)
    t_ms = None
    t_ms1 = None
    t_ms = kb.sig("pool", nc.gpsimd.memset(fq3[0][:, :], 0.0))
    t_ms2 = kb.sig("dve", nc.vector.memset(offs[:, :], 0.0))

    kb.wait("dve", sm.ticket())
    bh = bias_hi[:, :, :].rearrange("p a c -> p (a c)")
    blo = bias_lo[:, :, :].rearrange("p a c -> p (a c)")
    bld = DmaBuf(kb, "bias32")
    t_bias = None
    for ch in range(3):
        cs = slice(ch * 1024, (ch + 1) * 1024)
        kb.wait("sp", t_bias)
        bld.dma("sp", bias32[:, :], biasm[:, cs])
        kb.wait("dve", bld.ticket())
        t = kb.sig("dve", nc.vector.tensor_copy(out=bh[:, cs], in_=bias32[:, :]))
        kb.wait("dve", t)
        t = kb.sig("dve", nc.vector.tensor_tensor(out=bias32[:, :], in0=bias32[:, :], in1=bh[:, cs], op=ALU.subtract))
        kb.wait("dve", t)
        t_bias = kb.sig("dve", nc.vector.tensor_copy(out=blo[:, cs], in_=bias32[:, :]))

    if n_b > 0:
        t = kb.sig("dve", nc.vector.tensor_tensor(out=fl_sb[:, :], in0=fl_sb[:, :], in1=bf_sb[:, :], op=ALU.add))
        kb.wait("act", t)
        t = kb.sig("act", nc.scalar.activation(out=lsb[:, :], in_=fl_sb[:, :], func=AF.Exp, scale=-1.0))
        kb.wait("act", t)
        t_l = kb.sig("act", nc.scalar.activation(out=lsb[:, :], in_=lsb[:, :], func=AF.Ln, bias=1.0))
        kb.wait("pe", t_l, C["ticket"])
        b0, b1 = C["sbanks"][0], C["sbanks"][1]
        t_w = kb.sig("pe", nc.tensor.matmul(b0[:, 0:256], lhsT=C["U_f"][:, :], rhs=lsb[:, :], start=True, stop=True))
        t_t = kb.sig("pe", nc.tensor.matmul(b1[:, 0:256], lhsT=C["ones_f"][:, :], rhs=lsb[:, :], start=True, stop=True))
        kb.wait("dve", t_t, t_ms2)
        t = kb.sig("dve", nc.vector.tensor_copy(out=tot[:, :], in_=b1[:, 0:256]))
        o3 = offs[:, :].rearrange("p (h j) -> p h j", h=4)
        t3 = tot[:, :].rearrange("p (h j) -> p h j", h=4)
        for j in range(1, 64):
            kb.wait("dve", t)
            t = kb.sig("dve", nc.vector.tensor_tensor(out=o3[:, :, j], in0=o3[:, :, j - 1], in1=t3[:, :, j - 1], op=ALU.add))
        kb.wait("dve", t, t_w)
        t_f = kb.sig("dve", nc.vector.tensor_tensor(out=fneg[:, :], in0=offs[:, :], in1=b0[:, 0:256], op=ALU.add))
        st.s_free[0] = t_f
        st.s_free[1] = t
        kb.wait("dve", t_f)
        t = kb.sig("dve", nc.vector.tensor_scalar(out=parts[:, 0, :], in0=fneg[:, :], scalar1=-1.0, scalar2=None, op0=ALU.mult))
        kb.wait("dve", t)
        t = kb.sig("dve", nc.vector.scalar_tensor_tensor(out=r1[:, :], in0=fneg[:, :], scalar=-1.0, in1=parts[:, 0, :],
                                                          op0=ALU.mult, op1=ALU.subtract))
        kb.wait("dve", t)
        t = kb.sig("dve", nc.vector.tensor_copy(out=parts[:, 1, :], in_=r1[:, :]))
        kb.wait("dve", t)
        t = kb.sig("dve", nc.vector.tensor_tensor(out=r1[:, :], in0=r1[:, :], in1=parts[:, 1, :], op=ALU.subtract))
        kb.wait("dve", t)
        t_parts = kb.sig("dve", nc.vector.tensor_copy(out=parts[:, 2, :], in_=r1[:, :]))
        fs = DmaBuf(kb, "fscr")
        t_c = None
        for part in range(3):
            kb.wait("pe", t_parts, t_c)
            for h in range(4):
                tp = nc.tensor.transpose(out=C["tr_bank"][0:64, h * 128:(h + 1) * 128], in_=parts[:, part, h * 64:(h + 1) * 64],
                                         identity=C["ident_bf"][:, :])
            t_tp = kb.sig("pe", tp)
            kb.wait("act", t_tp)
            t_c = kb.sig("act", nc.scalar.copy(out=frT[:, part, :], in_=C["tr_bank"][0:64, :]))
            kb.wait("sp", t_c)
            fs.dma("sp", fscr[part].rearrange("h (j p) -> j h p", p=128), frT[:, part, :].rearrange("j (h p) -> j h p", p=128))
        C["tr_free"][1] = t_c
        t_fscr = fs.ticket()

    heads = [("A", h) for h in range(n_a)] + [("B", h) for h in range(n_b)]
    last_pv = {}

    def load_head(idx):
        kind, hl = heads[idx]
        b = idx % 2
        prev = last_pv.get(idx - 2)
        kb.wait("sp", prev)
        src = qkA if kind == "A" else qkB
        qkl[b].dma("sp", qbuf[b][:, :], src[0, hl * 128:(hl + 1) * 128, :])
        qkl[b].dma("sp", kbuf[b][:, :], src[1, hl * 128:(hl + 1) * 128, :])

    def load_fq3(idx):
        kind, hl = heads[idx]
        kb.wait("sp", last_pv.get(idx - 1), t_ms, t_ms1, t_fscr)
        fql.dma("sp", v1[0][:, :, :].rearrange("p j d -> p (j d)"), vb[hl])
        for part in range(3):
            fql.dma("sp", fq3[0][32 * part:32 * part + 1, :], fscr[part, hl:hl + 1, :])

    def load_vperm(idx):
        kind, hl = heads[idx]
        kb.wait("sp", last_pv.get(idx - 1))
        for p in range(3):
            vpl.dma("sp", vperm[p][:, :, :].rearrange("p b d -> p (b d)"), vaP[p, hl])

    load_head(0)
    for idx, (kind, hl) in enumerate(heads):
        b = idx % 2
        if kind == "A":
            load_vperm(idx)
        else:
            load_fq3(idx)
        if idx + 1 < len(heads):
            load_head(idx + 1)
        if kind == "A":
            emit_dilated_head(kb, st, nc, kT=kbuf[b], qT=qbuf[b], vperm=vperm, bias_hi=bias_hi, bias_lo=bias_lo, hl=hl,
                              ones_bf=C["ones_bf"], ident_bf=C["ident_bf"], dbanks=C["accs"][0] + C["accs"][1],
                              slot_free=kb.__dict__.setdefault("slot_free", [None] * 4), accN=accN, accZ=accZ,
                              ostage=dstage, o_dst=oT[hl * 128:(hl + 1) * 128, :],
                              op_waits=[qkl[b].ticket(), vpl.ticket(), C["ticket"], t_bias], rz=rz)
            C["acc_free"][0] = kb.dil_last
            C["acc_free"][1] = kb.dil_last
        else:
            emit_causal_head(kb, st, nc, kT=kbuf[b], qT=qbuf[b], v1=v1[0], fq3=fq3[0],
                             bias_col=(lambda j, hl=hl: fneg[:, hl * 64 + j:hl * 64 + j + 1]), ones_bf=C["ones_bf"],
                             ident_bf=C["ident_bf"], cmask_bf=C["cmask_bf"], accs=C["accs"], acc_free=C["acc_free"],
                             tr_bank=C["tr_bank"], tr_free=C["tr_free"], osb=C["osb"], ostage=C["ostage"],
                             o_dst=oT[512 + hl * 128:512 + (hl + 1) * 128, :],
                             op_waits=[qkl[b].ticket(), fql.ticket(), C["ticket"]], act_waits=[t_f],
                             rinv=C["rinv"])
            st.flush()
        last_pv[idx] = st.last_pv
    kb.wait("sp", *C["ostage"].all_tickets(), *dstage.all_tickets())
    kb.close()
    return nc


def t5_bucket_np(n):
    max_exact = 16
    nf = np.maximum(n, 1).astype(np.float32)
    large = max_exact + (np.log(nf / max_exact) / math.log(2048 / max_exact) * (32 - max_exact)).astype(np.int32)
    large = np.minimum(large, 31)
    return np.where(n < max_exact, n, large)


def attn_consts():
    ident = np.eye(128, dtype=np.float32)
    k = np.arange(128)[:, None]
    q = np.arange(128)[None, :]
    cmaskT = np.where(k <= q, 0.0, NEG).astype(np.float32)
    U = (k <= q).astype(np.float32)
    ones = np.ones((128, 128), np.float32)
    return np.stack([ident, cmaskT, U, ones])


def run_attn0(pre, rel_bias, b_forget, n_a=4, n_b=4):
    consts = attn_consts()
    dist = np.arange(129)
    kk = np.arange(128)[:, None]
    cc = np.arange(256)[None, :]
    dd = cc - kk
    valid = (dd >= 0) & (dd <= 128)
    in_maps = []
    for core in range(NCORES):
        b, g = core // 4, core % 4
        cores_b = [b * 4 + i for i in range(4)]
        qk = np.concatenate([pre[c]["qkT"] for c in cores_b], axis=1)
        v = np.concatenate([pre[c]["v"] for c in cores_b], axis=0)
        f = np.concatenate([pre[c]["fT"] for c in cores_b], axis=1)
        hs = slice(g * 512, (g + 1) * 512)
        qkA = np.stack([qk[0:2048][hs], qk[2048:4096][hs]])
        qkB = np.stack([qk[4096:6144][hs], qk[6144:8192][hs]])
        va = v[:, 0:2048][:, hs]
        vbb = v[:, 2048:4096][:, hs]
        vaP = np.stack([va.reshape(S // d, d, 512).transpose(1, 0, 2).reshape(64, 128, 4, 128).transpose(2, 1, 0, 3)
                        .reshape(4, 128, 64 * 128) for d in DILS])
        vb1 = np.ones((4, 128, 64, 129), NPBF)
        vb1[:, :, :, 0:128] = vbb.reshape(64, 128, 4, 128).transpose(2, 1, 0, 3)
        vbb = vb1.reshape(4, 128, 64 * 129)
        fl = f[g * 4:(g + 1) * 4].reshape(4, 64, 128).transpose(2, 0, 1).reshape(128, 256)
        bfr = np.broadcast_to(np.repeat(b_forget[g * 4:(g + 1) * 4], 64)[None, :], (128, 256))
        biasm = np.full((128, 12, 256), NEG, np.float32)
        for p, d in enumerate(DILS):
            bucket = t5_bucket_np(np.maximum(dd, 0) * d)
            for hl in range(4):
                vals = rel_bias[:, g * 4 + hl][bucket]
                biasm[:, p * 4 + hl, :] = np.where(valid, vals, NEG)
        in_maps.append({"qkA": np.ascontiguousarray(qkA), "vaP": np.ascontiguousarray(vaP),
                        "qkB": np.ascontiguousarray(qkB), "vb": np.ascontiguousarray(vbb),
                        "fl": np.ascontiguousarray(fl), "bfr": np.ascontiguousarray(bfr),
                        "biasm": np.ascontiguousarray(biasm.reshape(128, 12 * 256)), "consts": consts})
    nc = build_attn0(n_a, n_b)
    res = _launch(nc, in_maps)
    return res.results


def alloc_post(kb, tag, TH, nld=2):
    P = {}
    P["ol"] = [kb.sbuf(f"{tag}_o{i}", [128, TH], BF16) for i in range(nld)]
    P["zl"] = [kb.sbuf(f"{tag}_z{i}", [128, TH], BF16) for i in range(nld)]
    P["ld"] = [DmaBuf(kb, f"{tag}_oz{i}") for i in range(nld)]
    P["free"] = [None, None]
    P["xl"] = [kb.sbuf(f"{tag}_x{i}", [128, TH], F32) for i in range(2)]
    P["xld"] = [DmaBuf(kb, f"{tag}_x{i}") for i in range(2)]
    P["xfree"] = [None, None]
    P["stage"] = OutStage(kb, f"{tag}_st", [128, TH], F32)
    P["gbuf_free"] = None
    return P


def emit_post(kb, nc, P, *, oT, zT, xT, w_out, gate, x_out, gbuf, wbufs, banks, tag, toff, TH, evac_extra=None,
              bank_ids=None):
    ol, zl, ld, free = P["ol"], P["zl"], P["ld"], P["free"]
    tsl = slice(toff, toff + TH)
    g_ready = []
    for kc in range(KC):
        i = kc % len(ol)
        kb.wait("sp", free[i])
        ld[i].dma("sp", ol[i][:, :], oT[kc * 128:(kc + 1) * 128, tsl])
        ld[i].dma("sp", zl[i][:, :], zT[kc * 128:(kc + 1) * 128, tsl])
        e = "dve" if kc % 2 == 0 else "pool"
        kb.wait(e, ld[i].ticket(), P["gbuf_free"])
        eng = nc.vector if e == "dve" else nc.gpsimd
        free[i] = kb.sig(e, eng.tensor_tensor(out=gbuf[:, kc, :], in0=ol[i][:, :], in1=zl[i][:, :], op=ALU.mult))
        g_ready.append(free[i])
    xl, xld, stage = P["xl"], P["xld"], P["stage"]
    TT = TH // 512
    state = {}

    def evac(c, tt, bank, t_mm):
        i = c % 2
        if tt == 0:
            buf, t_prev = stage.next()
            state["buf"], state["prev"], state["tks"] = buf, t_prev, []
            kb.wait("sp", P["xfree"][i])
            xld[i].dma("sp", xl[i][:, :], xT[c * 128:(c + 1) * 128, tsl])
        sl = slice(tt * 512, (tt + 1) * 512)
        kb.wait("dve", t_mm, state["prev"], xld[i].ticket())
        tk = kb.sig("dve", nc.vector.scalar_tensor_tensor(out=state["buf"][:, sl], in0=bank[:, :], scalar=gate[:, c:c + 1],
                                                          in1=xl[i][:, sl], op0=ALU.mult, op1=ALU.add))
        state["tks"].append(tk)
        if evac_extra is not None:
            evac_extra(c, tt, state["buf"][:, sl], tk)
        if tt == TT - 1:
            P["xfree"][i] = tk
            if x_out is not None:
                stage.store(x_out[c * 128:(c + 1) * 128, tsl], state["buf"][:, :], *state["tks"])
        return tk

    kb.wait("pe", *g_ready[-2:])
    emit_linear_fm(kb, w_out, KC, gbuf, KC, banks, evac, tag + "_fm", wbufs=wbufs, bank_ids=bank_ids)
    P["gbuf_free"] = kb.bank_free_last_mm
    return stage.all_tickets()


TWO_PI = 2.0 * math.pi
C1_2PI = 6.28125
C2_2PI = TWO_PI - C1_2PI
QSC1 = 1.0 / math.sqrt(192.0)


def emit_rope_tables(kb, nc, pos_i, invf, cos_t, sin_t, tmp):
    ang, a, b = tmp
    V = nc.vector

    def step(instr, *w):
        return kb.sig("dve", instr)

    t = kb.sig("dve", V.tensor_copy(out=ang[:, :], in_=pos_i[:, :]))
    kb.wait("dve", t)
    t = kb.sig("dve", V.tensor_scalar(out=ang[:, :], in0=ang[:, :], scalar1=invf[:, 0:1], scalar2=None, op0=ALU.mult))
    kb.wait("dve", t)
    t = kb.sig("dve", V.tensor_scalar(out=a[:, :], in0=ang[:, :], scalar1=1.0 / TWO_PI, scalar2=None, op0=ALU.mult))
    kb.wait("dve", t)
    t = kb.sig("dve", V.tensor_copy(out=pos_i[:, :], in_=a[:, :]))
    kb.wait("dve", t)
    t = kb.sig("dve", V.tensor_copy(out=a[:, :], in_=pos_i[:, :]))
    kb.wait("dve", t)
    t = kb.sig("dve", V.scalar_tensor_tensor(out=b[:, :], in0=a[:, :], scalar=-C1_2PI, in1=ang[:, :], op0=ALU.mult, op1=ALU.add))
    kb.wait("dve", t)
    t = kb.sig("dve", V.scalar_tensor_tensor(out=b[:, :], in0=a[:, :], scalar=-C2_2PI, in1=b[:, :], op0=ALU.mult, op1=ALU.add))
    kb.wait("dve", t)
    t = kb.sig("dve", V.tensor_single_scalar(out=a[:, :], in_=b[:, :], scalar=math.pi, op=ALU.is_gt))
    kb.wait("dve", t)
    t = kb.sig("dve", V.scalar_tensor_tensor(out=b[:, :], in0=a[:, :], scalar=-TWO_PI, in1=b[:, :], op0=ALU.mult, op1=ALU.add))
    kb.wait("dve", t)
    t = kb.sig("dve", V.tensor_scalar(out=b[:, :], in0=b[:, :], scalar1=-math.pi, scalar2=math.pi, op0=ALU.max, op1=ALU.min))
    kb.wait("act", t)
    t_sin = kb.sig("act", nc.scalar.activation(out=sin_t[:, :], in_=b[:, :], func=AF.Sin))
    kb.wait("dve", t)
    t = kb.sig("dve", V.tensor_scalar(out=a[:, :], in0=b[:, :], scalar1=math.pi / 2, scalar2=None, op0=ALU.add))
    kb.wait("dve", t)
    t = kb.sig("dve", V.tensor_single_scalar(out=ang[:, :], in_=a[:, :], scalar=math.pi, op=ALU.is_gt))
    kb.wait("dve", t)
    t = kb.sig("dve", V.scalar_tensor_tensor(out=a[:, :], in0=ang[:, :], scalar=-TWO_PI, in1=a[:, :], op0=ALU.mult, op1=ALU.add))
    kb.wait("dve", t)
    t = kb.sig("dve", V.tensor_scalar(out=a[:, :], in0=a[:, :], scalar1=-math.pi, scalar2=math.pi, op0=ALU.max, op1=ALU.min))
    kb.wait("act", t)
    t_cos = kb.sig("act", nc.scalar.activation(out=cos_t[:, :], in_=a[:, :], func=AF.Sin))
    return t_sin, t_cos


def alloc_pre1(kb, TH):
    B = {}
    B["cqg"] = kb.sbuf("cqg", [128, 8, TH], BF16)
    B["ckvg"] = kb.sbuf("ckvg", [128, 4, TH], BF16)
    B["rq"] = kb.sbuf("rq", [128, TH], F32)
    B["rkv"] = kb.sbuf("rkv", [128, TH], F32)
    B["rkvc"] = kb.sbuf("rkvc", [128, TH // 128], F32)
    B["cos"] = kb.sbuf("cos_t", [128, TH], F32)
    B["sin"] = kb.sbuf("sin_t", [128, TH], F32)
    B["sqb"] = [kb.sbuf(f"sqb{i}", [128, 512], BF16) for i in range(2)]
    B["sq_free"] = [None, None]
    B["stage"] = OutStage(kb, "p1st", [128, TH], BF16)
    B["sd"] = kb.sbuf("p1sd", [128, 512], F32)
    B["kr"] = kb.sbuf("kr", [32, 4, 512], F32)
    B["kro"] = OutStage(kb, "kro", [32, 2, TH], BF16, n=1)
    B["krw"] = [kb.sbuf(f"krw{i}", [128, KC * 32], BF16) for i in range(2)]
    B["crs"] = kb.sbuf("crs", [128, 2, TH], F32)
    B["rtmp"] = [kb.sbuf(f"rtmp{i}", [128, 512], F32) for i in range(2)]
    B["rst"] = OutStage(kb, "rst", [128, 2, TH], BF16, n=1)
    B["vstage"] = OutStage(kb, "p1vst", [128, TH // 128, 256], BF16, n=1)
    return B


def emit_pre1(kb, nc, B, *, x1T, hT, A1, B1, banks, wbufs, ones_bf, ones_f, gq, gkv, invf, pos_dram, W, OUT, toff, TH, x_ready):
    TT = TH // 512
    tsl_all = slice(toff, toff + TH)
    tg = f"p1_{toff}"
    V = nc.vector
    cqg, ckvg, rq, rkv, rkvc, cos_t, sin_t, sqb, stage, sd = (B[k] for k in
                                                              ("cqg", "ckvg", "rq", "rkv", "rkvc", "cos", "sin", "sqb", "stage", "sd"))
    kb.wait("sp", *x_ready)
    with contextlib.ExitStack() as sub:
        kb_es, kb.es = kb.es, sub
        h_ready = emit_norm(kb, x1T[:, tsl_all], hT, ones_bf, A1, B1, banks[0], tg + "n")
        kb.es = kb_es
    kb.bank_free[0] = kb.norm_bank_ticket
    with contextlib.ExitStack() as sub:
        kb_es, kb.es = kb.es, sub
        pos_i = kb.sbuf(tg + "posi", [128, TH], I32)
        tmp = [kb.sbuf(tg + f"rt{i}", [128, TH], F32) for i in range(3)]
        pl = DmaBuf(kb, tg + "pos")
        kb.wait("sp", h_ready[-1])
        pl.dma("sp", pos_i[:, :], pos_dram[:, tsl_all])
        kb.wait("dve", pl.ticket(), h_ready[-1])
        t_sin, t_cos = emit_rope_tables(kb, nc, pos_i, invf, cos_t, sin_t, tmp)
        kb.es = kb_es
    st = {"pend": [], "sqn": 0}

    def flush_pend():
        for fn in st["pend"]:
            fn()
        st["pend"] = []

    def evac_lat(c, tt, bank, t_mm):
        flush_pend()
        isq = c < 8
        dst = cqg if isq else ckvg
        cc = c if isq else c - 8
        g = gq if isq else gkv
        sl = slice(tt * 512, (tt + 1) * 512)
        i = st["sqn"] % 2
        st["sqn"] += 1
        kb.wait("act", t_mm, B["sq_free"][i])
        kb.sig("act", nc.scalar.activation(out=dst[:, cc, sl], in_=bank[:, :], func=AF.Copy, scale=g[:, cc:cc + 1]))
        tk = kb.sig("act", nc.scalar.activation(out=sqb[i][:, :], in_=bank[:, :], func=AF.Square))
        sbi = (4 if isq else 6) + tt
        first, last = (cc == 0), (cc == (7 if isq else 3))

        def mm(i=i, tk=tk, sbi=sbi, first=first, last=last):
            kb.wait("pe", tk)
            if first:
                kb.wait("pe", kb.bank_free[sbi])
            t2 = kb.sig("pe", nc.tensor.matmul(banks[sbi][:, :], lhsT=ones_bf[:, :], rhs=sqb[i][:, :], start=first, stop=last))
            B["sq_free"][i] = t2
            st["last_ss"] = t2
        st["pend"].append(mm)
        return tk

    emit_linear_fm(kb, W["w1"], 12, hT, KC, banks, evac_lat, tg + "lat", h_ready=h_ready, wbufs=wbufs, bank_ids=[0, 1, 2, 3])
    flush_pend()
    t_r = None
    for which, (rt, nfeat, b0) in enumerate(((rq, 1024.0, 4), (rkv, 512.0, 6))):
        for tt in range(TT):
            kb.wait("act", st["last_ss"], t_r)
            t = kb.sig("act", nc.scalar.activation(out=sd[:, :], in_=banks[b0 + tt][:, :], func=AF.Sqrt, scale=1.0 / nfeat, bias=EPS))
            kb.bank_free[b0 + tt] = t
            kb.wait("dve", t)
            t_r = kb.sig("dve", nc.vector.reciprocal(out=rt[:, tt * 512:(tt + 1) * 512], in_=sd[:, :]))
    kb.wait("pe", t_r, kb.bank_free[4])
    for t8 in range(TH // 128):
        mm = nc.tensor.matmul(banks[4][:, t8:t8 + 1], lhsT=rkv[0:1, t8 * 128:(t8 + 1) * 128], rhs=ones_f[0:1, 0:1],
                              start=(t8 == 0), stop=True)
    t = kb.sig("pe", mm)
    kb.wait("dve", t)
    t_rc = kb.sig("dve", nc.vector.tensor_copy(out=rkvc[:, :], in_=banks[4][:, 0:TH // 128]))
    kb.bank_free[4] = t_rc

    kr, kro = B["kr"], B["kro"]
    ks = {}

    def evac_kr(c, tt, bank, t_mm):
        sl = slice(tt * 512, (tt + 1) * 512)
        if c == 0:
            kb.wait("dve", t_mm)
            tk = kb.sig("dve", V.tensor_copy(out=kr[:, tt, :], in_=bank[0:32, :]))
            ks[tt] = tk
            return tk
        if tt == 0:
            buf, t_prev = kro.next()
            ks["buf"], ks["prev"], ks["tks"] = buf, t_prev, []
        buf = ks["buf"]
        x1 = kr[:, tt, :]
        ta, tb = kr[:, 2, :], kr[:, 3, :]
        x2 = bank[0:32, :]
        c32, s32 = cos_t[0:32, sl], sin_t[0:32, sl]
        kb.wait("dve", t_mm, ks[tt], t_sin, t_cos, ks["prev"])
        t = kb.sig("dve", V.tensor_tensor(out=ta, in0=x1, in1=c32, op=ALU.mult))
        t = kb.sig("dve", V.tensor_tensor(out=tb, in0=x2, in1=s32, op=ALU.mult))
        kb.wait("dve", t)
        t = kb.sig("dve", V.tensor_tensor(out=buf[:, 0, sl], in0=ta, in1=tb, op=ALU.subtract))
        kb.wait("dve", t)
        t = kb.sig("dve", V.tensor_tensor(out=ta, in0=x1, in1=s32, op=ALU.mult))
        t = kb.sig("dve", V.tensor_tensor(out=tb, in0=x2, in1=c32, op=ALU.mult))
        kb.wait("dve", t)
        tk = kb.sig("dve", V.tensor_tensor(out=buf[:, 1, sl], in0=ta, in1=tb, op=ALU.add))
        ks["tks"].append(tk)
        if tt == TT - 1:
            kro.store(OUT["krT"][:, tsl_all].rearrange("(h p) t -> p h t", p=32), buf[:, :, :], *ks["tks"])
        return tk

    emit_linear_fm(kb, W["w_kr"], 2, hT, KC, banks, evac_kr, tg + "kr", mrows=32, wbufs=B["krw"], bank_ids=[0, 1, 2, 3])

    def staged(dst_dram, fn):
        s2 = {}

        def evac(c, tt, bank, t_mm):
            if tt == 0:
                buf, t_prev = stage.next()
                s2["buf"], s2["prev"], s2["tks"] = buf, t_prev, []
            sl = slice(tt * 512, (tt + 1) * 512)
            tk = fn(c, tt, bank, t_mm, s2["buf"][:, sl], s2["prev"], sl)
            s2["tks"].append(tk)
            if tt == TT - 1:
                stage.store(dst_dram[c * 128:(c + 1) * 128, tsl_all], s2["buf"][:, :], *s2["tks"])
            return tk
        return evac

    def f_z(c, tt, bank, t_mm, out, prev, sl):
        kb.wait("act", t_mm, prev)
        return kb.sig("act", nc.scalar.activation(out=out, in_=bank[:, :], func=AF.Silu))
    emit_linear_fm(kb, W["w_z"], 32, hT, KC, banks, staged(OUT["zT"], f_z), tg + "z", wbufs=wbufs, bank_ids=[0, 1, 2, 3])

    def f_qn(c, tt, bank, t_mm, out, prev, sl):
        kb.wait("dve", t_mm, prev, t_r)
        return kb.sig("dve", V.scalar_tensor_tensor(out=out, in0=bank[:, :], scalar=QSC1, in1=rq[:, sl],
                                                    op0=ALU.mult, op1=ALU.mult))
    emit_linear_fm(kb, W["w_qn"], 32, cqg, 8, banks, staged(OUT["qnT"], f_qn), tg + "qn", wbufs=wbufs, bank_ids=[0, 1, 2, 3],
                   group=4)

    crs, rtmp, rst = B["crs"], B["rtmp"], B["rst"]
    kb.wait("dve", t_cos, t_sin, t_r, getattr(kb, "crs_free", None))
    kb.sig("dve", V.scalar_tensor_tensor(out=crs[:, 0, :], in0=cos_t[:, :], scalar=QSC1, in1=rq[:, :], op0=ALU.mult, op1=ALU.mult))
    t_crs = kb.sig("dve", V.scalar_tensor_tensor(out=crs[:, 1, :], in0=sin_t[:, :], scalar=QSC1, in1=rq[:, :], op0=ALU.mult, op1=ALU.mult))
    rs = {}

    def evac_qr(c, tt, bank, t_mm):
        sl = slice(tt * 512, (tt + 1) * 512)
        if c % 2 == 0:
            rs[("r1", tt)] = (bank, t_mm)
            return None
        b1, t1 = rs[("r1", tt)]
        if tt == 0:
            buf, t_prev = rst.next()
            rs["buf"], rs["prev"], rs["tks"] = buf, t_prev, []
        buf = rs["buf"]
        kb.wait("dve", t1, t_mm, rs["prev"], t_crs)
        a, b = rtmp
        t = kb.sig("dve", V.tensor_tensor(out=a[:, :], in0=b1[:, :], in1=crs[:, 0, sl], op=ALU.mult))
        t = kb.sig("dve", V.tensor_tensor(out=b[:, :], in0=bank[:, :], in1=crs[:, 1, sl], op=ALU.mult))
        kb.wait("dve", t)
        t = kb.sig("dve", V.tensor_tensor(out=buf[:, 0, sl], in0=a[:, :], in1=b[:, :], op=ALU.subtract))
        kb.wait("dve", t)
        t = kb.sig("dve", V.tensor_tensor(out=a[:, :], in0=b1[:, :], in1=crs[:, 1, sl], op=ALU.mult))
        t = kb.sig("dve", V.tensor_tensor(out=b[:, :], in0=bank[:, :], in1=crs[:, 0, sl], op=ALU.mult))
        kb.wait("dve", t)
        tk = kb.sig("dve", V.tensor_tensor(out=buf[:, 1, sl], in0=a[:, :], in1=b[:, :], op=ALU.add))
        rs["tks"].append(tk)
        kb.bank_free[((c - 1) * TT + tt) % 4] = tk
        kb.crs_free = tk
        if tt == TT - 1:
            g = c // 2
            dst = OUT["qrT"][g * 256:(g + 1) * 256, tsl_all].rearrange("(h p) t -> p h t", p=128)
            rst.store(dst, buf[:, :, :], *rs["tks"])
        return tk

    emit_linear_fm(kb, W["w_qr"], 16, cqg, 8, banks, evac_qr, tg + "qr", wbufs=wbufs, bank_ids=[0, 1, 2, 3], group=4)

    def f_kn(c, tt, bank, t_mm, out, prev, sl):
        kb.wait("dve", t_mm, prev, t_r)
        return kb.sig("dve", V.tensor_tensor(out=out, in0=bank[:, :], in1=rkv[:, sl], op=ALU.mult))
    emit_linear_fm(kb, W["w_kn"], 32, ckvg, 4, banks, staged(OUT["knT"], f_kn), tg + "kn", wbufs=wbufs, bank_ids=[0, 1, 2, 3],
                   group=8)

    vstage = B["vstage"]
    vs = {}
    NTC = TH // 128

    def evac_v(j, t, bank, t_mm):
        if t == 0:
            buf, t_prev = vstage.next()
            vs["buf"], vs["prev"], vs["tks"] = buf, t_prev, []
        kb.wait("act", t_mm, vs["prev"], t_rc)
        tk = kb.sig("act", nc.scalar.activation(out=vs["buf"][:, t, :], in_=bank[:, 0:256], func=AF.Copy, scale=rkvc[:, t:t + 1]))
        vs["tks"].append(tk)
        if t == NTC - 1:
            dst = OUT["v"][toff:toff + TH, j * 256:(j + 1) * 256].rearrange("(t p) c -> p t c", p=128)
            vstage.store(dst, vs["buf"][:, :, :], *vs["tks"])
        return tk

    emit_linear_tm(kb, W["w_v"], 16, ckvg, 4, 256, banks, evac_v, tg + "v", wbufs, bank_ids=[0, 1, 2, 3])
    return B["kro"].all_tickets() + stage.all_tickets() + rst.all_tickets() + vstage.all_tickets()


TH = 1024


def build_mid():
    nc = bass.Bass("TRN2", target_bir_lowering=False)
    dt = nc.dram_tensor
    oT0 = dt("oT0", [D, T], BF16, kind="ExternalInput").ap()
    zT0 = dt("zT0", [D, T], BF16, kind="ExternalInput").ap()
    xT = dt("xT", [D, T], F32, kind="ExternalInput").ap()
    vecs = dt("vecs", [128, 96 + 96 + 32 + 8 + 4 + 1], F32, kind="ExternalInput").ap()
    pos = dt("pos", [128, T], I32, kind="ExternalInput").ap()
    ones_in = dt("ones", [128, 128], F32, kind="ExternalInput").ap()
    w_out0 = dt("w_out0", [KC, 128, KC * 128], F32, kind="ExternalInput").ap()
    W = {"w1": dt("w1", [12, 128, KC * 128], F32, kind="ExternalInput").ap(),
         "w_kr": dt("w_kr", [2, 128, KC * 32], F32, kind="ExternalInput").ap(),
         "w_z": dt("w_z", [32, 128, KC * 128], F32, kind="ExternalInput").ap(),
         "w_qn": dt("w_qn", [32, 128, 8 * 128], F32, kind="ExternalInput").ap(),
         "w_qr": dt("w_qr", [16, 128, 8 * 128], F32, kind="ExternalInput").ap(),
         "w_kn": dt("w_kn", [32, 128, 4 * 128], F32, kind="ExternalInput").ap(),
         "w_v": dt("w_v", [16, 128, 4 * 256], F32, kind="ExternalInput").ap()}
    x1T = dt("x1T", [D, T], F32, kind="ExternalOutput").ap()
    OUT = {"zT": dt("zT1", [D, T], BF16, kind="ExternalOutput").ap(),
           "qnT": dt("qnT", [D, T], BF16, kind="ExternalOutput").ap(),
           "qrT": dt("qrT", [2048, T], BF16, kind="ExternalOutput").ap(),
           "knT": dt("knT", [D, T], BF16, kind="ExternalOutput").ap(),
           "krT": dt("krT", [64, T], BF16, kind="ExternalOutput").ap(),
           "v": dt("v1", [T, D], BF16, kind="ExternalOutput").ap()}
    kb = KB(nc)
    banks = [kb.psum(f"bank{i}", [128, 512]) for i in range(8)]
    vsb = kb.sbuf("vsb", [128, 237], F32)
    ones_bf = kb.sbuf("ones_bf", [128, 128], BF16)
    ones_f = kb.sbuf("ones_f", [128, 128], F32)
    A1 = kb.sbuf("A1", [128, KC], F32)
    t_c = load_consts(kb, [(vsb[:], vecs), (ones_f[:], ones_in)])
    dc = DmaBuf(kb, "ones")
    dc.dma("pool", ones_bf[:], ones_in)
    mod0, mod1, g1 = vsb[:, 0:96], vsb[:, 96:192], vsb[:, 192:224]
    gq, gkv, invf = vsb[:, 224:232], vsb[:, 232:236], vsb[:, 236:237]
    kb.wait("dve", t_c)
    t_A = kb.sig("dve", nc.vector.scalar_tensor_tensor(out=A1[:], in0=mod1[:, 32:64], scalar=1.0, in1=g1, op0=ALU.add, op1=ALU.mult))
    kb.wait("act", t_c, t_A)
    kb.wait("pe", dc.ticket(), t_c)
    kb.wait("pool", t_c)
    wbufs = [kb.sbuf(f"wb{i}", [128, KC * 128], BF16) for i in range(2)]
    with contextlib.ExitStack() as sub:
        kb_es, kb.es = kb.es, sub
        gbuf = kb.sbuf("gbufA", [128, KC, T], BF16)
        P = alloc_post(kb, "pa", T, nld=1)
        kb.es = kb_es
        x_ready = emit_post(kb, nc, P, oT=oT0, zT=zT0, xT=xT, w_out=w_out0, gate=mod0[:, 64:96], x_out=x1T, gbuf=gbuf,
                            wbufs=wbufs, banks=banks, tag="pa", toff=0, TH=T)
    kb.barrier(*x_ready)
    hT = kb.sbuf("hT", [128, KC, TH], BF16)
    B = alloc_pre1(kb, TH)
    tks = []
    for half in range(T // TH):
        tks += emit_pre1(kb, nc, B, x1T=x1T, hT=hT, A1=A1, B1=mod1[:, 0:32], banks=banks, wbufs=wbufs, ones_bf=ones_bf,
                         ones_f=ones_f, gq=gq, gkv=gkv, invf=invf, pos_dram=pos, W=W, OUT=OUT, toff=half * TH, TH=TH,
                         x_ready=x_ready)
    kb.wait("sp", *x_ready, *tks)
    kb.close()
    return nc


def mid_weights(w_out0, w_in_odd, w_q_b, w_kv_b):
    Wd = {"w_out0": lay_fm(w_out0, 128)}
    Wd["w1"] = lay_fm(w_in_odd[:, 0:1536], 128)
    Wd["w_kr"] = lay_fm(w_in_odd[:, 1536:1600], 32)
    Wd["w_z"] = lay_fm(w_in_odd[:, 1600:5696], 128)
    wq = w_q_b.reshape(1024, 32, 192)
    Wd["w_qn"] = lay_fm(np.ascontiguousarray(wq[:, :, 0:128]).reshape(1024, 4096), 128)
    r1 = wq[:, :, 128:160].reshape(1024, 8, 128)
    r2 = wq[:, :, 160:192].reshape(1024, 8, 128)
    Wd["w_qr"] = lay_fm(np.stack([r1, r2], axis=2).reshape(1024, 2048), 128)
    wkv = w_kv_b.reshape(512, 32, 256)
    Wd["w_kn"] = lay_fm(np.ascontiguousarray(wkv[:, :, 0:128]).reshape(512, 4096), 128)
    Wd["w_v"] = lay_fm(np.ascontiguousarray(wkv[:, :, 128:256]).reshape(512, 4096), 256)
    return Wd


def rope_invf():
    half = 32
    inv = (np.float32(10000.0) ** (-(np.arange(half, dtype=np.float32) / np.float32(half)))).astype(np.float32)
    return np.tile(inv, 4).reshape(128, 1)


def run_mid(oT_list, zT_list, xT_list, mod, g1, gq, gkv, positions, Wd):
    ones = np.ones((128, 128), np.float32)
    invf = rope_invf()
    in_maps = []
    for core in range(NCORES):
        b, q = core // 4, core % 4
        vecs = np.concatenate([colT(mod[0][b]), colT(mod[1][b]), colT(g1), colT(gq), colT(gkv), invf], axis=1)
        pos = np.ascontiguousarray(np.broadcast_to(positions[b, q * T:(q + 1) * T][None, :], (128, T))).astype(np.int32)
        m = {"oT0": oT_list[core], "zT0": zT_list[core], "xT": xT_list[core], "vecs": np.ascontiguousarray(vecs),
             "pos": pos, "ones": ones}
        m.update(Wd)
        in_maps.append(m)
    nc = build_mid()
    res = _launch(nc, in_maps)
    return res.results


def build_attn1(n_h=8):
    nc = bass.Bass("TRN2", target_bir_lowering=False)
    qn = nc.dram_tensor("qn", [8, 128, S], BF16, kind="ExternalInput").ap()
    qr = nc.dram_tensor("qr", [8, 64, S], BF16, kind="ExternalInput").ap()
    kn = nc.dram_tensor("kn", [8, 128, S], BF16, kind="ExternalInput").ap()
    kr = nc.dram_tensor("kr", [64, S], BF16, kind="ExternalInput").ap()
    vv = nc.dram_tensor("vv", [8, 128, 64 * 129], BF16, kind="ExternalInput").ap()
    consts_in = nc.dram_tensor("consts", [4, 128, 128], F32, kind="ExternalInput").ap()
    oT = nc.dram_tensor("oT", [1024, S], BF16, kind="ExternalOutput").ap()
    kb = KB(nc)
    C = attn_common(kb, nc, consts_in)
    st = C["st"]
    qnb = [kb.sbuf(f"qnb{i}", [128, S], BF16) for i in range(2)]
    knb = [kb.sbuf(f"knb{i}", [128, S], BF16) for i in range(2)]
    qrb = [kb.sbuf(f"qrb{i}", [64, S], BF16) for i in range(2)]
    v1 = [kb.sbuf(f"v1_{i}", [128, 64, 129], BF16) for i in range(2)]
    krb = kb.sbuf("krb", [64, S], BF16)
    ld = [DmaBuf(kb, f"ld{i}") for i in range(2)]
    kl = DmaBuf(kb, "kl")
    kl.dma("sp", krb[:, :], kr)
    last_pv = {}

    def load_head(h):
        b = h % 2
        kb.wait("sp", last_pv.get(h - 2))
        ld[b].dma("sp", qnb[b][:, :], qn[h])
        ld[b].dma("sp", knb[b][:, :], kn[h])
        ld[b].dma("sp", qrb[b][:, :], qr[h])
        ld[b].dma("sp", v1[b][:, :, :].rearrange("p j d -> p (j d)"), vv[h])

    load_head(0)
    for h in range(n_h):
        b = h % 2
        if h + 1 < n_h:
            load_head(h + 1)
        emit_causal_head(kb, st, nc, kT=knb[b], qT=qnb[b], v1=v1[b], kr=krb, qr=qrb[b], ident_bf=C["ident_bf"],
                         cmask_bf=C["cmask_bf"], accs=C["accs"], acc_free=C["acc_free"], tr_bank=C["tr_bank"],
                         tr_free=C["tr_free"], osb=C["osb"], ostage=C["ostage"], o_dst=oT[h * 128:(h + 1) * 128, :],
                         op_waits=[ld[b].ticket(), kl.ticket(), C["ticket"]], act_waits=[], rinv=C["rinv"])
        st.flush()
        last_pv[h] = st.last_pv
    kb.wait("sp", *C["ostage"].all_tickets())
    kb.close()
    return nc


def gather_tok(res, key, b):
    return np.concatenate([res[b * 4 + i][key] for i in range(4)], axis=1)


def run_attn1(mid, n_h=8):
    consts = attn_consts()
    in_maps = []
    per_b = {}
    for b in range(NB):
        qnT = gather_tok(mid, "qnT", b)
        qrT = gather_tok(mid, "qrT", b)
        knT = gather_tok(mid, "knT", b)
        krT = gather_tok(mid, "krT", b)
        v = np.concatenate([mid[b * 4 + i]["v1"] for i in range(4)], axis=0)
        qr_h = qrT.reshape(8, 2, 4, 32, S).transpose(0, 2, 1, 3, 4).reshape(32, 64, S)
        per_b[b] = (qnT.reshape(32, 128, S), qr_h, knT.reshape(32, 128, S), krT, v)
    for core in range(NCORES):
        b, g = core // 4, core % 4
        qn_, qr_, kn_, kr_, v = per_b[b]
        hs = slice(g * 8, (g + 1) * 8)
        v1 = np.ones((8, 128, 64, 129), NPBF)
        v1[:, :, :, 0:128] = v[:, g * 1024:(g + 1) * 1024].reshape(64, 128, 8, 128).transpose(2, 1, 0, 3)
        in_maps.append({"qn": np.ascontiguousarray(qn_[hs]), "qr": np.ascontiguousarray(qr_[hs]),
                        "kn": np.ascontiguousarray(kn_[hs]), "kr": np.ascontiguousarray(kr_),
                        "vv": v1.reshape(8, 128, 64 * 129), "consts": consts})
    nc = build_attn1(n_h)
    res = _launch(nc, in_maps)
    return res.results


def build_post1():
    nc = bass.Bass("TRN2", target_bir_lowering=False)
    dt = nc.dram_tensor
    oT1 = dt("oT1", [D, T], BF16, kind="ExternalInput").ap()
    zT1 = dt("zT1", [D, T], BF16, kind="ExternalInput").ap()
    x1T = dt("x1T", [D, T], F32, kind="ExternalInput").ap()
    vecs = dt("vecs", [128, 96 + 32], F32, kind="ExternalInput").ap()
    ones_in = dt("ones", [128, 128], F32, kind="ExternalInput").ap()
    w_out1 = dt("w_out1", [KC, 128, KC * 128], F32, kind="ExternalInput").ap()
    outT = dt("outT", [D, T], F32, kind="ExternalOutput").ap()
    x2T = dt("x2T", [D, T], F32).ap()
    kb = KB(nc)
    banks = [kb.psum(f"bank{i}", [128, 512]) for i in range(8)]
    vsb = kb.sbuf("vsb", [128, 128], F32)
    ones_bf = kb.sbuf("ones_bf", [128, 128], BF16)
    t_c = load_consts(kb, [(vsb[:], vecs)])
    dc = DmaBuf(kb, "ones")
    dc.dma("pool", ones_bf[:], ones_in)
    mod1, gf = vsb[:, 0:96], vsb[:, 96:128]
    kb.wait("dve", t_c)
    kb.wait("act", t_c)
    kb.wait("pe", dc.ticket())
    TP = T
    gbuf = kb.sbuf("gbuf", [128, KC, TP], BF16)
    wbufs = [kb.sbuf(f"wb{i}", [128, KC * 128], BF16) for i in range(2)]
    P = alloc_post(kb, "pb", TP, nld=1)
    sqb = [kb.sbuf(f"sqb{i}", [128, 512], BF16) for i in range(2)]
    sq_free = [None, None]
    sd = kb.sbuf("sd", [128, 512], F32)
    rstd = kb.sbuf("rstd", [128, TP], F32)
    TT = TP // 512
    V = nc.vector
    st = {"pend": [], "n": 0}

    def flush_pend():
        for fn in st["pend"]:
            fn()
        st["pend"] = []

    def extra(c, tt, xnew, tk):
        flush_pend()
        i = st["n"] % 2
        st["n"] += 1
        kb.wait("act", tk, sq_free[i])
        t2 = kb.sig("act", nc.scalar.activation(out=sqb[i][:, :], in_=xnew, func=AF.Square))

        def mm(i=i, t2=t2, c=c, tt=tt):
            kb.wait("pe", t2)
            t3 = kb.sig("pe", nc.tensor.matmul(banks[4 + tt][:, :], lhsT=ones_bf[:, :], rhs=sqb[i][:, :],
                                               start=(c == 0), stop=(c == KC - 1)))
            sq_free[i] = t3
            st["last"] = t3
        st["pend"].append(mm)

    stores = emit_post(kb, nc, P, oT=oT1, zT=zT1, xT=x1T, w_out=w_out1, gate=mod1[:, 64:96], x_out=x2T, gbuf=gbuf,
                       wbufs=wbufs, banks=banks, tag="pb", toff=0, TH=TP, evac_extra=extra, bank_ids=[0, 1, 2, 3])
    flush_pend()
    t_r = None
    for tt in range(TT):
        kb.wait("act", st["last"], t_r)
        t = kb.sig("act", nc.scalar.activation(out=sd[:, :], in_=banks[4 + tt][:, :], func=AF.Sqrt, scale=1.0 / D, bias=EPS))
        kb.wait("dve", t)
        t_r = kb.sig("dve", V.reciprocal(out=rstd[:, tt * 512:(tt + 1) * 512], in_=sd[:, :]))
    xl, xld, stage = P["xl"], P["xld"], P["stage"]
    kb.wait("sp", *stores)
    for c in range(KC):
        i = c % 2
        kb.wait("sp", P["xfree"][i])
        xld[i].dma("sp", xl[i][:, :], x2T[c * 128:(c + 1) * 128, :])
        buf, t_prev = stage.next()
        kb.wait("dve", xld[i].ticket(), t_prev, t_r)
        t_fin = kb.sig("dve", V.scalar_tensor_tensor(out=buf[:, :], in0=xl[i][:, :], scalar=gf[:, c:c + 1], in1=rstd[:, :],
                                                      op0=ALU.mult, op1=ALU.mult))
        P["xfree"][i] = t_fin
        stage.store(outT[c * 128:(c + 1) * 128, :], buf[:, :], t_fin)
    kb.wait("sp", *stage.all_tickets())
    kb.close()
    return nc


def run_post1(oT_list, zT_list, x1T_list, mod1, g_final, w_out1_l):
    ones = np.ones((128, 128), np.float32)
    in_maps = []
    for core in range(NCORES):
        b = core // 4
        vecs = np.concatenate([colT(mod1[b]), colT(g_final)], axis=1)
        in_maps.append({"oT1": oT_list[core], "zT1": zT_list[core], "x1T": x1T_list[core],
                        "vecs": np.ascontiguousarray(vecs), "ones": ones, "w_out1": w_out1_l})
    nc = build_post1()
    res = _launch(nc, in_maps)
    return res.results


def scatter_heads_to_tok(attn_res, b):
    full = np.concatenate([attn_res[b * 4 + g]["oT"] for g in range(4)], axis=0)
    return full


def kernel(x, c, positions, g_norm, w_ada, b_ada, rel_bias, w_in_even, b_forget, w_out_even, w_in_odd, g_q_lora,
           g_kv_lora, w_q_b, w_kv_b, w_out_odd, g_final):
    x = np.asarray(x, np.float32)
    mod = run_mod(np.asarray(c), np.asarray(w_ada), np.asarray(b_ada))
    pre = run_pre0(x, mod[0], np.asarray(g_norm)[0], np.asarray(w_in_even)[0])
    a0 = run_attn0(pre, np.asarray(rel_bias), np.asarray(b_forget)[0])
    xs = tok_shard_T(x)
    oT0, zT0 = [], []
    for core in range(NCORES):
        b, q = core // 4, core % 4
        tsl = slice(q * T, (q + 1) * T)
        oa = np.concatenate([a0[b * 4 + g]["oT"][0:512, tsl] for g in range(4)], axis=0)
        ob = np.concatenate([a0[b * 4 + g]["oT"][512:1024, tsl] for g in range(4)], axis=0)
        oT0.append(np.ascontiguousarray(np.concatenate([oa, ob], axis=0)))
        zT0.append(pre[core]["zT"])
    Wd = mid_weights(np.asarray(w_out_even)[0], np.asarray(w_in_odd)[0], np.asarray(w_q_b)[0], np.asarray(w_kv_b)[0])
    mid = run_mid(oT0, zT0, xs, mod, np.asarray(g_norm)[1], np.asarray(g_q_lora)[0], np.asarray(g_kv_lora)[0],
                  np.asarray(positions), Wd)
    a1 = run_attn1(mid)
    oT1 = []
    for core in range(NCORES):
        b, q = core // 4, core % 4
        tsl = slice(q * T, (q + 1) * T)
        oT1.append(np.ascontiguousarray(np.concatenate([a1[b * 4 + g]["oT"][:, tsl] for g in range(4)], axis=0)))
    fin = run_post1(oT1, [m["zT1"] for m in mid], [m["x1T"] for m in mid], mod[1], np.asarray(g_final),
                    lay_fm(np.asarray(w_out_odd)[0], 128))
    out = np.empty((NB, S, D), np.float32)
    for core in range(NCORES):
        b, q = core // 4, core % 4
        out[b, q * T:(q + 1) * T, :] = fin[core]["outT"].T
    if DEBUG.get("stash") is not None:
        DEBUG["stash"].update(dict(mod=mod, pre=pre, a0=a0, mid=mid, a1=a1, oT0=oT0, oT1=oT1))
    return out
```
